# Optimizing a Trainium2 kernel written in Bass

```python
import jax, jax.numpy as jnp
from jax import lax
import numpy as np

D_MODEL = 2048
BATCH = 4
SEQ = 2048
DEPTH = 4

N_A_LAYERS = DEPTH // 2
N_B_LAYERS = DEPTH - N_A_LAYERS
RWKV_HEAD = 64
RWKV_HEADS = D_MODEL // RWKV_HEAD
D_DECAY_LORA = max(32, int(round(1.8 * D_MODEL ** 0.5 / 32)) * 32)
D_AAA_LORA = max(32, int(round(1.8 * D_MODEL ** 0.5 / 32)) * 32)
D_MV_LORA = max(32, int(round(1.3 * D_MODEL ** 0.5 / 32)) * 32)
D_GATE_LORA = max(32, int(round(0.6 * D_MODEL ** 0.8 / 32)) * 32)
GN_EPS = 64e-5
ATT_HEAD = 128
Q_HEADS = D_MODEL // ATT_HEAD
KV_HEADS = 4
GROUP = Q_HEADS // KV_HEADS
MOBA_BLOCK = 256
MOBA_TOPK = 3
Q_CHUNK = 16
ROPE_THETA = 10000.0
D_FF = 4 * D_MODEL
RMS_EPS = 1e-6

kernel_name = 'yoco_rwkv7_moba_hybrid'


def rms_norm(x, g):
    xf = x.astype(jnp.float32)
    y = xf * lax.rsqrt(jnp.mean(xf * xf, axis=-1, keepdims=True) + RMS_EPS)
    return (y * g.astype(jnp.float32)).astype(x.dtype)


def rope(x, positions):
    half = x.shape[-1] // 2
    inv = ROPE_THETA ** (-jnp.arange(half, dtype=jnp.float32) / half)
    ang = positions.astype(jnp.float32)[:, None] * inv[None, :]
    cos = jnp.cos(ang)[None, :, None, :]
    sin = jnp.sin(ang)[None, :, None, :]
    xf = x.astype(jnp.float32)
    x1, x2 = xf[..., :half], xf[..., half:]
    return jnp.concatenate([x1 * cos - x2 * sin, x2 * cos + x1 * sin], axis=-1).astype(x.dtype)


def sq_relu_mlp(x, w1, w2):
    return jnp.square(jax.nn.relu(x @ w1)) @ w2


def wkv7_scan(r, decay, k, v, a, b):
    B, T, H, N = r.shape

    def step(S, inp):
        r_t, w_t, k_t, v_t, a_t, b_t = inp
        sa = jnp.einsum('bhij,bhj->bhi', S, a_t)
        S = (S * w_t[:, :, None, :] + sa[..., None] * b_t[:, :, None, :]
             + v_t[..., None] * k_t[:, :, None, :])
        y_t = jnp.einsum('bhij,bhj->bhi', S, r_t)
        return S, y_t

    xs = tuple(jnp.moveaxis(z, 1, 0) for z in (r, decay, k, v, a, b))
    S0 = jnp.zeros((B, H, N, N), jnp.float32)
    _, y = lax.scan(step, S0, xs)
    return jnp.moveaxis(y, 0, 1)


def rwkv7_time_mix(x, v_first, mu, w_rkv, w0, w1, w2, a0, a1, a2, g1, g2,
                   k_k, k_a, r_k, gn_w, gn_b, w_o, v_lora):
    B, T, D = x.shape
    H, N = RWKV_HEADS, RWKV_HEAD
    xx = jnp.pad(x, ((0, 0), (1, 0), (0, 0)))[:, :-1] - x
    xr, xw, xk, xv, xa, xg = [x + xx * mu[i] for i in range(6)]
    rkv = jnp.einsum('nbtd,nde->nbte', jnp.stack([xr, xk, xv]), w_rkv)
    r, k, v = rkv[0], rkv[1], rkv[2]
    w = -jax.nn.softplus(-(w0 + jnp.tanh(xw @ w1) @ w2)) - 0.5
    if v_lora is None:
        v_first = v
    else:
        v0, v1, v2 = v_lora
        v = v + (v_first - v) * jax.nn.sigmoid(v0 + (xv @ v1) @ v2)
    a = jax.nn.sigmoid(a0 + (xa @ a1) @ a2)
    g = jax.nn.sigmoid(xg @ g1) @ g2
    f32 = jnp.float32
    kk = (k * k_k).astype(f32).reshape(B, T, H, N)
    kk = kk / jnp.maximum(jnp.sqrt(jnp.sum(kk * kk, axis=-1, keepdims=True)), 1e-12)
    k = k * (1.0 + (a - 1.0) * k_a)
    rh = r.astype(f32).reshape(B, T, H, N)
    kh = k.astype(f32).reshape(B, T, H, N)
    vh = v.astype(f32).reshape(B, T, H, N)
    ah = a.astype(f32).reshape(B, T, H, N)
    decay = jnp.exp(-jnp.exp(w.astype(f32))).reshape(B, T, H, N)
    y = wkv7_scan(rh, decay, kh, vh, -kk, kk * ah)
    mean = jnp.mean(y, axis=-1, keepdims=True)
    var = jnp.mean(jnp.square(y - mean), axis=-1, keepdims=True)
    y = ((y - mean) * lax.rsqrt(var + GN_EPS)).reshape(B, T, D)
    y = y * gn_w.astype(f32) + gn_b.astype(f32)
    bonus = jnp.sum(rh * kh * r_k.astype(f32), axis=-1, keepdims=True) * vh
    y = y + bonus.reshape(B, T, D)
    out = (y.astype(x.dtype) * g) @ w_o
    return out, v_first


def shared_kv(h, g_kv, w_kv):
    B, T, _ = h.shape
    hn = rms_norm(h, g_kv)
    kv = hn @ w_kv
    k, v = jnp.split(kv, 2, axis=-1)
    k = rope(k.reshape(B, T, KV_HEADS, ATT_HEAD), jnp.arange(T))
    v = v.reshape(B, T, KV_HEADS, ATT_HEAD)
    nb = -(-T // MOBA_BLOCK)
    pad = nb * MOBA_BLOCK - T
    k_blk = jnp.pad(k, ((0, 0), (0, pad), (0, 0), (0, 0))).reshape(
        B, nb, MOBA_BLOCK, KV_HEADS, ATT_HEAD).transpose(0, 3, 1, 2, 4)
    v_blk = jnp.pad(v, ((0, 0), (0, pad), (0, 0), (0, 0))).reshape(
        B, nb, MOBA_BLOCK, KV_HEADS, ATT_HEAD).transpose(0, 3, 1, 2, 4)
    k_mean = jnp.mean(k_blk.astype(jnp.float32), axis=3)
    return k_blk, v_blk, k_mean


def moba_attention(q, k_blk, v_blk, k_mean):
    B, T, _, hd = q.shape
    nb = k_blk.shape[2]
    topk = min(MOBA_TOPK, nb)
    scale = hd ** -0.5
    f32 = jnp.float32
    qg = q.reshape(B, T, KV_HEADS, GROUP, hd).transpose(0, 2, 3, 1, 4)
    q_blk = jnp.arange(T) // MOBA_BLOCK
    gate = jnp.einsum('bkgtd,bknd->bkgtn', qg.astype(f32), k_mean)
    past = jnp.arange(nb)[None, :] < q_blk[:, None]
    gate = jnp.where(past, gate, -jnp.inf)
    _, sel = lax.top_k(gate, topk)
    slot_ok = jnp.arange(topk)[None, :] < q_blk[:, None]

    tp = -(-T // Q_CHUNK) * Q_CHUNK
    pad = tp - T
    nc = tp // Q_CHUNK
    qg = jnp.pad(qg, ((0, 0), (0, 0), (0, 0), (0, pad), (0, 0)))
    sel = jnp.pad(sel, ((0, 0), (0, 0), (0, 0), (0, pad), (0, 0)))
    slot_ok = jnp.pad(slot_ok, ((0, pad), (0, 0)))
    q_c = qg.reshape(B, KV_HEADS, GROUP, nc, Q_CHUNK, hd).transpose(3, 0, 1, 2, 4, 5)
    sel_c = sel.reshape(B, KV_HEADS, GROUP, nc, Q_CHUNK, topk).transpose(3, 0, 1, 2, 4, 5)
    ok_c = slot_ok.reshape(nc, Q_CHUNK, topk)
    starts = jnp.arange(nc) * Q_CHUNK
    k_flat = k_blk.reshape(B, KV_HEADS, nb * MOBA_BLOCK, hd)
    v_flat = v_blk.reshape(B, KV_HEADS, nb * MOBA_BLOCK, hd)
    bi = jnp.arange(B)[:, None, None, None, None]
    hi = jnp.arange(KV_HEADS)[None, :, None, None, None]

    def chunk(args):
        qc, sc, okc, s0 = args
        kg = k_blk[bi, hi, sc]
        vg = v_blk[bi, hi, sc]
        own0 = (s0 // MOBA_BLOCK) * MOBA_BLOCK
        k_own = lax.dynamic_slice_in_dim(k_flat, own0, MOBA_BLOCK, axis=2)
        v_own = lax.dynamic_slice_in_dim(v_flat, own0, MOBA_BLOCK, axis=2)
        qk = qc.astype(kg.dtype)
        l_sel = jnp.einsum('bkgcd,bkgcsnd->bkgcsn', qk, kg, preferred_element_type=f32) * scale
        l_sel = jnp.where(okc[None, None, None, :, :, None], l_sel, -jnp.inf)
        l_own = jnp.einsum('bkgcd,bknd->bkgcn', qk, k_own, preferred_element_type=f32) * scale
        qpos = s0 + jnp.arange(Q_CHUNK)
        kpos = own0 + jnp.arange(MOBA_BLOCK)
        l_own = jnp.where(kpos[None, :] <= qpos[:, None], l_own, -jnp.inf)
        nsel = topk * MOBA_BLOCK
        logits = jnp.concatenate([l_sel.reshape(B, KV_HEADS, GROUP, Q_CHUNK, nsel), l_own], axis=-1)
        p = jax.nn.softmax(logits, axis=-1)
        p_sel = p[..., :nsel].reshape(B, KV_HEADS, GROUP, Q_CHUNK, topk, MOBA_BLOCK)
        p_own = p[..., nsel:]
        o = (jnp.einsum('bkgcsn,bkgcsnd->bkgcd', p_sel.astype(vg.dtype), vg, preferred_element_type=f32)
             + jnp.einsum('bkgcn,bknd->bkgcd', p_own.astype(v_own.dtype), v_own, preferred_element_type=f32))
        return o.astype(qc.dtype)

    out = lax.map(chunk, (q_c, sel_c, ok_c, starts))
    out = out.transpose(1, 0, 4, 2, 3, 5).reshape(B, tp, Q_HEADS * hd)
    return out[:, :T]


def moba_layer(hn, k_blk, v_blk, k_mean, w_q, w_o):
    B, T, _ = hn.shape
    q = rope((hn @ w_q).reshape(B, T, Q_HEADS, ATT_HEAD), jnp.arange(T))
    return moba_attention(q, k_blk, v_blk, k_mean) @ w_o


def setup_inputs(seed: int = 0) -> dict:
    key = jax.random.key(seed)
    ks = iter(jax.random.split(key, 48))
    f32 = jnp.float32
    D, NA, NBL = D_MODEL, N_A_LAYERS, N_B_LAYERS
    NV = max(NA - 1, 0)

    def nrm(shape, scale):
        return jax.random.normal(next(ks), shape, f32) * scale

    def gain(shape):
        return 1.0 + 0.02 * jax.random.normal(next(ks), shape, f32)

    return {
        'x': nrm((BATCH, SEQ, D), 1.0),
        'ln_mix_g': gain((DEPTH, D)),
        'ln_ffn_g': gain((DEPTH, D)),
        'w_ff1': nrm((DEPTH, D, D_FF), D ** -0.5),
        'w_ff2': nrm((DEPTH, D_FF, D), D_FF ** -0.5),
        'rw_mu': jax.random.uniform(next(ks), (NA, 6, D), f32),
        'rw_w_rkv': nrm((NA, 3, D, D), D ** -0.5),
        'rw_w0': jax.random.uniform(next(ks), (NA, D), f32, -6.0, -1.0),
        'rw_w1': nrm((NA, D, D_DECAY_LORA), D ** -0.5),
        'rw_w2': nrm((NA, D_DECAY_LORA, D), D_DECAY_LORA ** -0.5),
        'rw_a0': nrm((NA, D), 0.1),
        'rw_a1': nrm((NA, D, D_AAA_LORA), D ** -0.5),
        'rw_a2': nrm((NA, D_AAA_LORA, D), D_AAA_LORA ** -0.5),
        'rw_g1': nrm((NA, D, D_GATE_LORA), D ** -0.5),
        'rw_g2': nrm((NA, D_GATE_LORA, D), D_GATE_LORA ** -0.5),
        'rw_k_k': 0.85 + nrm((NA, D), 0.05),
        'rw_k_a': 1.0 + nrm((NA, D), 0.05),
        'rw_r_k': nrm((NA, RWKV_HEADS, RWKV_HEAD), 0.1),
        'rw_gn_w': gain((NA, D)),
        'rw_gn_b': nrm((NA, D), 0.02),
        'rw_w_o': nrm((NA, D, D), D ** -0.5),
        'rw_v0': nrm((NV, D), 0.1),
        'rw_v1': nrm((NV, D, D_MV_LORA), D ** -0.5),
        'rw_v2': nrm((NV, D_MV_LORA, D), D_MV_LORA ** -0.5),
        'kv_norm_g': gain((D,)),
        'w_kv': nrm((D, 2 * KV_HEADS * ATT_HEAD), D ** -0.5),
        'mb_w_q': nrm((NBL, D, Q_HEADS * ATT_HEAD), D ** -0.5),
        'mb_w_o': nrm((NBL, Q_HEADS * ATT_HEAD, D), (Q_HEADS * ATT_HEAD) ** -0.5),
        'final_g': gain((D,)),
    }


def reference(x, ln_mix_g, ln_ffn_g, w_ff1, w_ff2,
              rw_mu, rw_w_rkv, rw_w0, rw_w1, rw_w2, rw_a0, rw_a1, rw_a2,
              rw_g1, rw_g2, rw_k_k, rw_k_a, rw_r_k, rw_gn_w, rw_gn_b, rw_w_o,
              rw_v0, rw_v1, rw_v2, kv_norm_g, w_kv, mb_w_q, mb_w_o, final_g):
    h = x
    v_first = None
    kv = None
    for layer in range(DEPTH):
        hn = rms_norm(h, ln_mix_g[layer])
        if layer < N_A_LAYERS:
            i = layer
            v_lora = None if i == 0 else (rw_v0[i - 1], rw_v1[i - 1], rw_v2[i - 1])
            mix, v_first = rwkv7_time_mix(
                hn, v_first, rw_mu[i], rw_w_rkv[i], rw_w0[i], rw_w1[i], rw_w2[i],
                rw_a0[i], rw_a1[i], rw_a2[i], rw_g1[i], rw_g2[i], rw_k_k[i], rw_k_a[i],
                rw_r_k[i], rw_gn_w[i], rw_gn_b[i], rw_w_o[i], v_lora)
        else:
            if kv is None:
                kv = shared_kv(h, kv_norm_g, w_kv)
            j = layer - N_A_LAYERS
            mix = moba_layer(hn, kv[0], kv[1], kv[2], mb_w_q[j], mb_w_o[j])
        h = h + mix
        h = h + sq_relu_mlp(rms_norm(h, ln_ffn_g[layer]), w_ff1[layer], w_ff2[layer])
    return rms_norm(h, final_g)
```

```python
from contextlib import ExitStack
import numpy as np
import concourse.bass as bass
import concourse.mybir as mybir

F32 = mybir.dt.float32
BF16 = mybir.dt.bfloat16
ALU = mybir.AluOpType
AF = mybir.ActivationFunctionType
AX = mybir.AxisListType

ENGS = ("pe", "act", "dve", "pool", "sp")


class R:
    __slots__ = ("ap", "name", "last_w", "reads", "dsem", "dcnt", "is_psum")

    def __init__(self, ap, name=""):
        self.ap = ap
        self.name = name
        self.last_w = None
        self.reads = []
        self.dsem = None
        self.dcnt = 0
        self.is_psum = False

    def __getitem__(self, idx):
        return self.ap[idx]


class Ctx:
    def __init__(self, nc):
        self.nc = nc
        self.es = ExitStack()
        self.eng = {"pe": nc.tensor, "act": nc.scalar, "dve": nc.vector,
                    "pool": nc.gpsimd, "sp": nc.sync}
        self.sem = {k: self.es.enter_context(nc.semaphore("s_" + k)) for k in ENGS}
        self.ops = {k: [] for k in ENGS}
        self.seen = {k: {} for k in ENGS}
        self.nsem = 0
        self.out_events = []
        self.all_dma = {}
        self._psum_ids = set()

    def sbuf(self, name, shape, dtype):
        t = self.es.enter_context(self.nc.sbuf_tensor(name, list(shape), dtype))
        return t

    def psum(self, name, shape, dtype=F32):
        t = self.es.enter_context(self.nc.psum_tensor(name, list(shape), dtype))
        self._psum_ids.add(id(t))
        return t

    def reg(self, ap, name=""):
        r = R(ap, name)
        r.is_psum = id(ap) in self._psum_ids
        return r

    def alias(self, new_regions, old_regions):
        evs = []
        for o in old_regions:
            if o.last_w is not None:
                evs.append(o.last_w)
            evs.extend(o.reads)
        for n in new_regions:
            n.reads = list(evs) + n.reads
            n.is_psum = n.is_psum or any(o.is_psum for o in old_regions)

    def _newsem(self, name):
        self.nsem += 1
        return self.es.enter_context(self.nc.semaphore("d_%d" % self.nsem))

    def _collect(self, eng, reads, writes):
        waits = {}

        def need(ev, raw):
            if ev is None:
                return
            key, val = ev
            if key == eng and eng == "pe" and not raw:
                return
            if waits.get(key, 0) < val:
                waits[key] = val

        for t in reads:
            need(t.last_w, True)
            if t.is_psum:
                for ev in t.reads:
                    if ev[0] != eng:
                        need(ev, True)
        for t in writes:
            need(t.last_w, False)
            for ev in t.reads:
                need(ev, False)
        wl = []
        for key, val in waits.items():
            if self.seen[eng].get(key, 0) >= val:
                continue
            self.seen[eng][key] = val
            wl.append((key, val))
            if isinstance(key, str):
                self.ops[key][val - 1][2] = True
        return wl

    def op(self, eng, fn, reads=(), writes=(), **kw):
        if isinstance(fn, str):
            name, kws = fn, kw
            fn = lambda e, name=name, kws=kws: getattr(e, name)(**kws)
        wl = self._collect(eng, reads, writes)
        self.ops[eng].append([wl, fn, False, "c"])
        ev = (eng, len(self.ops[eng]))
        for t in reads:
            t.reads.append(ev)
        for t in writes:
            t.last_w = ev
            t.reads = []
        return ev

    def dma(self, q, out, in_, reads=(), writes=(), semreg=None, is_out=False, **kw):
        wl = self._collect(q, reads, writes)
        if semreg is None:
            semreg = writes[0] if writes else reads[0]
        if semreg.dsem is None:
            semreg.dsem = self._newsem(semreg.name)
        semreg.dcnt += 16
        sem = semreg.dsem
        self.ops[q].append([wl, (out, in_, kw, sem), False, "d"])
        self.all_dma[sem] = semreg.dcnt
        ev = (sem, semreg.dcnt)
        for t in reads:
            t.reads.append(ev)
        for t in writes:
            t.last_w = ev
            t.reads = []
        if is_out:
            self.out_events.append(ev)
        return ev

    def wait_all_outputs(self, eng="sp"):
        waits = dict(self.all_dma)
        self.ops[eng].append([list(waits.items()), None, False, "w"])

    def finalize(self):
        nc = self.nc
        semval = {}
        for k in ENGS:
            c = 0
            vals = []
            for o in self.ops[k]:
                if o[3] == "c" and o[2]:
                    c += 1
                vals.append(c)
            semval[k] = vals
        ctx = self

        def run(k, e):
            for i, (wl, fn, flagged, kind) in enumerate(ctx.ops[k]):
                for key, val in wl:
                    if isinstance(key, str):
                        e.wait_ge(ctx.sem[key], semval[key][val - 1])
                    else:
                        e.wait_ge(key, val)
                if kind == "c":
                    ins = fn(e)
                    if flagged:
                        ins.then_inc(ctx.sem[k], 1)
                elif kind == "d":
                    out, in_, kw, sem = fn
                    e.dma_start(out=out, in_=in_, **kw).then_inc(sem, 16)

        with nc.Block() as block:
            @block.sync
            def _(e):
                run("sp", e)

            @block.scalar
            def _(e):
                run("act", e)

            @block.vector
            def _(e):
                run("dve", e)

            @block.gpsimd
            def _(e):
                run("pool", e)

            @block.tensor
            def _(e):
                run("pe", e)
        self.es.close()


D = 2048
NC16 = 16
NT = 1024
TT = 512
NTT = NT // TT
DFF = 8192
RMS_EPS = 1e-6


class WStream:
    def __init__(self, c, nslot=3, q="pool"):
        self.c = c
        self.q = q
        self.slots = []
        for i in range(nslot):
            t = c.sbuf("wslot%d" % i, [128, 16, 512], BF16)
            self.slots.append(c.reg(t, "wslot%d" % i))
        self.i = 0

    def load(self, w_ap, row0, col0, nrows=2048, ncols=512):
        s = self.slots[self.i % len(self.slots)]
        self.i += 1
        src = w_ap[row0:row0 + nrows, col0:col0 + ncols].rearrange("(c p) f -> p c f", p=128)
        nck = nrows // 128
        self.c.dma(self.q, s[:, 0:nck, 0:ncols], src, writes=[s])
        return s


class PsumPool:
    def __init__(self, c, n=8, prefix="ps"):
        self.banks = [c.reg(c.psum("%s%d" % (prefix, i), [128, 512]), "%s%d" % (prefix, i)) for i in range(n)]
        self.i = 0

    def next(self):
        b = self.banks[self.i % len(self.banks)]
        self.i += 1
        return b


def emit_rmsnorm(c, pp, resid, sq, ones, gcol, rstd, out, tmp, in_place=False):
    for tt in range(NTT):
        ts = slice(tt * TT, (tt + 1) * TT)
        for ch in range(NC16):
            c.op("act", "activation", out=sq[ch][tt][:], in_=resid[ch][:, ts], func=AF.Square,
                 reads=[resid[ch]], writes=[sq[ch][tt]])
        ps = pp.next()
        for ch in range(NC16):
            c.op("pe", "matmul", out=ps[:], lhsT=ones[:], rhs=sq[ch][tt][:], start=(ch == 0), stop=(ch == NC16 - 1),
                 reads=[ones, sq[ch][tt]], writes=[ps])
        c.op("dve", "tensor_scalar", out=tmp[:], in0=ps[:], scalar1=1.0 / D, scalar2=RMS_EPS, op0=ALU.mult, op1=ALU.add,
             reads=[ps], writes=[tmp])
        c.op("act", "activation", out=tmp[:], in_=tmp[:], func=AF.Sqrt, reads=[tmp], writes=[tmp])
        c.op("dve", "reciprocal", out=rstd[tt][:], in_=tmp[:], reads=[tmp], writes=[rstd[tt]])
        for ch in range(NC16):
            if in_place:
                c.op("dve", "scalar_tensor_tensor", out=resid[ch][:, ts], in0=resid[ch][:, ts], scalar=gcol[:, ch:ch + 1], in1=rstd[tt][:], op0=ALU.mult, op1=ALU.mult,
                     reads=[resid[ch], gcol, rstd[tt]], writes=[resid[ch]])
            else:
                c.op("dve", "scalar_tensor_tensor", out=out[ch][tt][:], in0=resid[ch][:, ts], scalar=gcol[:, ch:ch + 1], in1=rstd[tt][:], op0=ALU.mult, op1=ALU.mult,
                     reads=[resid[ch], gcol, rstd[tt]], writes=[out[ch][tt]])


def emit_proj_add(c, pp, ws, w_ap, row0, src, resid, nk=16):
    for og in range(4):
        slot = ws.load(w_ap, row0, og * 512, nrows=nk * 128)
        for ocl in range(4):
            oc = og * 4 + ocl
            for tt in range(NTT):
                ts = slice(tt * TT, (tt + 1) * TT)
                ps = pp.next()
                for k in range(nk):
                    c.op("pe", "matmul", out=ps[:], lhsT=slot[:, k, ocl * 128:(ocl + 1) * 128], rhs=src[k][tt][:], start=(k == 0), stop=(k == nk - 1),
                         reads=[slot, src[k][tt]], writes=[ps])
                c.op("dve", "tensor_tensor", out=resid[oc][:, ts], in0=ps[:], in1=resid[oc][:, ts], op=ALU.add,
                     reads=[ps, resid[oc]], writes=[resid[oc]])


def emit_ffn(c, pp, ws, w1_ap, w2_ap, xn, hid, resid, rtmp):
    for q in range(4):
        for fg in range(4):
            slot = ws.load(w1_ap, 0, q * 2048 + fg * 512)
            for fcl in range(4):
                fc = fg * 4 + fcl
                for tt in range(NTT):
                    ps = pp.next()
                    for k in range(NC16):
                        c.op("pe", "matmul", out=ps[:], lhsT=slot[:, k, fcl * 128:(fcl + 1) * 128], rhs=xn[k][tt][:], start=(k == 0), stop=(k == NC16 - 1),
                             reads=[slot, xn[k][tt]], writes=[ps])
                    rt = rtmp[(fc * NTT + tt) % len(rtmp)]
                    c.op("act", "activation", out=rt[:], in_=ps[:], func=AF.Relu, reads=[ps], writes=[rt])
                    c.op("dve", "tensor_tensor", out=hid[fc][tt][:], in0=rt[:], in1=rt[:], op=ALU.mult,
                         reads=[rt], writes=[hid[fc][tt]])
        emit_proj_add(c, pp, ws, w2_ap, q * 2048, hid, resid)


def build_ff(final=False):
    nc = bass.Bass("TRN2", target_bir_lowering=False)
    hT = nc.dram_tensor("hT", [D, NT], F32, kind="ExternalInput").ap()
    mT = nc.dram_tensor("mT", [D, NT], BF16, kind="ExternalInput").ap()
    wo = nc.dram_tensor("wo", [D, D], F32, kind="ExternalInput").ap()
    w1 = nc.dram_tensor("w1", [D, DFF], F32, kind="ExternalInput").ap()
    w2 = nc.dram_tensor("w2", [DFF, D], F32, kind="ExternalInput").ap()
    g = nc.dram_tensor("g", [128, 32], F32, kind="ExternalInput").ap()
    ones_in = nc.dram_tensor("ones", [128, 128], BF16, kind="ExternalInput").ap()
    oT = nc.dram_tensor("oT", [D, NT], F32, kind="ExternalOutput").ap()

    c = Ctx(nc)
    resid_t = c.sbuf("resid", [128, NC16, NT], F32)
    resid = [c.reg(resid_t[:, ch, :], "resid%d" % ch) for ch in range(NC16)]
    xn_t = c.sbuf("xn", [128, NC16, NT], BF16)
    xn = [[c.reg(xn_t[:, ch, tt * TT:(tt + 1) * TT], "xn") for tt in range(NTT)] for ch in range(NC16)]
    big_t = c.sbuf("big", [128, NC16, NT], BF16)
    big = [[c.reg(big_t[:, ch, tt * TT:(tt + 1) * TT], "big") for tt in range(NTT)] for ch in range(NC16)]
    gcol = c.reg(c.sbuf("gcol", [128, 32], F32), "gcol")
    ones = c.reg(c.sbuf("ones_sb", [128, 128], BF16), "ones")
    rstd = [c.reg(c.sbuf("rstd%d" % tt, [128, TT], F32), "rstd") for tt in range(NTT)]
    tmp = c.reg(c.sbuf("ntmp", [128, TT], F32), "ntmp")
    rtmp = [c.reg(c.sbuf("rtmp%d" % i, [128, TT], F32), "rtmp") for i in range(2)]
    ws = WStream(c, nslot=3)
    pp = PsumPool(c, 8)

    c.dma("sp", gcol[:], g, writes=[gcol])
    c.dma("sp", ones[:], ones_in, writes=[ones])
    hv = hT.rearrange("(c p) t -> p c t", p=128)
    for ch in range(NC16):
        c.dma("sp" if ch % 2 == 0 else "act", resid[ch][:], hv[:, ch, :], writes=[resid[ch]])
    mv = mT.rearrange("(c p) t -> p c t", p=128)
    for ch in range(NC16):
        c.dma("act" if ch % 2 == 0 else "sp", big_t[:, ch, :], mv[:, ch, :], writes=[big[ch][0], big[ch][1]])

    emit_proj_add(c, pp, ws, wo, 0, big, resid)
    gl = c.reg(gcol[:, 0:16], "gl")
    gl.last_w = gcol.last_w
    emit_rmsnorm(c, pp, resid, big, ones, gcol, rstd, xn, tmp)
    emit_ffn(c, pp, ws, w1, w2, xn, big, resid, rtmp)
    if final:
        gf = c.reg(gcol[:, 16:32], "gf")
        gf.last_w = gcol.last_w
        emit_rmsnorm(c, pp, resid, big, ones, gf, rstd, None, tmp, in_place=True)
    ov = oT.rearrange("(c p) t -> p c t", p=128)
    for ch in range(NC16):
        c.dma("sp" if ch % 2 == 0 else "act", ov[:, ch, :], resid[ch][:], reads=[resid[ch]], is_out=True)
    c.wait_all_outputs("sp")
    c.wait_all_outputs("act")
    c.finalize()
    return nc


FL = 1024
NFC = 8
STW = 256
CH = 128
NJ = STW // CH
GN_EPS = 64e-5
WSCALE = -0.6065306597126334

P_W0, P_A0, P_V0, P_KK, P_KA, P_RK, P_GW, P_GB, P_OMKA = range(9)
MU_ORDER = {"r": 0, "w": 1, "k": 2, "v": 3, "a": 4, "g": 5}


def build_rw(layer, T=2048, stop=99):
    NST = T // STW
    nc = bass.Bass("TRN2", target_bir_lowering=False)

    def din(name, shape, dt=F32):
        return nc.dram_tensor(name, list(shape), dt, kind="ExternalInput").ap()

    hT = din("hT", [D, T])
    gmix_d = din("gmix", [128, 16])
    mu_d = din("mu", [128, 96])
    wr_d, wk_d, wv_d = din("wr", [D, FL]), din("wk", [D, FL]), din("wv", [D, FL])
    w1_d, a1_d, g1_d = din("w1", [D, 96]), din("a1", [D, 96]), din("g1", [D, 256])
    w2_d, a2_d, g2_d = din("w2", [96, FL]), din("a2", [96, FL]), din("g2", [256, FL])
    pv_d = din("pv", [128, 72])
    if layer == 1:
        v1_d, v2_d = din("v1", [D, 64]), din("v2", [64, FL])
        vfT = din("vfT", [FL, T])
    ident_d = din("ident", [128, 128], BF16)
    onesb_d = din("onesb", [128, 128], BF16)
    bones_d = din("bones", [128, 128], F32)
    onesf_d = din("onesf", [128, 128], F32)
    mask3_d = din("mask3", [128, 384], BF16)
    masksu_d = din("masksu", [128, 128], BF16)
    masksl_d = din("masksl", [128, 128], BF16)
    ygT = nc.dram_tensor("ygT", [FL, T], BF16, kind="ExternalOutput").ap()
    if layer == 0:
        vT_o = nc.dram_tensor("vT", [FL, T], F32, kind="ExternalOutput").ap()

    c = Ctx(nc)

    def sb(name, shape, dt):
        return c.sbuf("sb_" + name, shape, dt)

    def rg(ap, name=""):
        return c.reg(ap, name)

    ident = rg(sb("ident", [128, 128], BF16), "ident")
    onesb = rg(sb("onesb", [128, 128], BF16), "onesb")
    bones = rg(sb("bones", [128, 128], F32), "bones")
    onesf = rg(sb("onesf", [128, 128], F32), "onesf")
    mask3 = rg(sb("mask3", [128, 384], BF16), "mask3")
    masksu = rg(sb("masksu", [128, 128], BF16), "masksu")
    masksl = rg(sb("masksl", [128, 128], BF16), "masksl")
    gmix = rg(sb("gmix", [128, 16], F32), "gmix")
    mu = rg(sb("mu", [128, 96], F32), "mu")
    omm = rg(sb("omm", [128, 96], F32), "omm")
    pv = rg(sb("pv", [128, 72], F32), "pv")
    for r_, d_ in ((ident, ident_d), (onesb, onesb_d), (bones, bones_d), (onesf, onesf_d), (mask3, mask3_d),
                   (masksu, masksu_d), (masksl, masksl_d), (gmix, gmix_d), (mu, mu_d), (pv, pv_d)):
        c.dma("sp", r_[:], d_, writes=[r_])
    c.op("dve", "tensor_scalar", out=omm[:], in0=mu[:], scalar1=-1.0, scalar2=1.0, op0=ALU.mult, op1=ALU.add, reads=[mu], writes=[omm])
    c.op("dve", "tensor_scalar", out=pv[:, P_OMKA * 8:(P_OMKA + 1) * 8], in0=pv[:, P_KA * 8:(P_KA + 1) * 8], scalar1=-1.0, scalar2=1.0,
         op0=ALU.mult, op1=ALU.add, reads=[pv], writes=[pv])

    def pcol(p, fc):
        return pv[:, p * 8 + fc:p * 8 + fc + 1]

    w2sb = rg(sb("w2sb", [96, FL], BF16), "w2sb")
    a2sb = rg(sb("a2sb", [96, FL], BF16), "a2sb")
    g2sb = rg(sb("g2sb", [128, 2, FL], BF16), "g2sb")
    c.dma("pool", w2sb[:], w2_d, writes=[w2sb])
    c.dma("pool", a2sb[:], a2_d, writes=[a2sb])
    c.dma("pool", g2sb[:], g2_d.rearrange("(m p) f -> p m f", p=128), writes=[g2sb])
    if layer == 1:
        v2sb = rg(sb("v2sb", [64, FL], BF16), "v2sb")
        c.dma("pool", v2sb[:], v2_d, writes=[v2sb])

    hn_t = sb("hn", [128, 16, STW + 1], F32)
    hn = [rg(hn_t[:, ch, :], "hn%d" % ch) for ch in range(16)]
    carry_t = sb("carry", [128, 16, 1], F32)
    carry = rg(carry_t, "carry")
    xbuf_t = [sb("xbuf%d" % i, [128, 16, STW], BF16) for i in range(2)]
    xbuf = [[rg(xbuf_t[i][:, ch, :], "x%d_%d" % (i, ch)) for ch in range(16)] for i in range(2)]
    rstd = rg(sb("rstd", [128, STW], F32), "rstd")

    def fcbuf(name, dt):
        t = sb(name, [128, NFC, STW], dt)
        return t, [rg(t[:, fc, :], "%s%d" % (name, fc)) for fc in range(NFC)]

    a_tt, a_t = fcbuf("a_t", BF16)
    r_tt, r_t = fcbuf("r_t", BF16)
    Epos_t, Epos = fcbuf("Epos", BF16)
    Eneg_t, Eneg = fcbuf("Eneg", BF16)
    Eprev_t, Eprev = fcbuf("Eprev", BF16)
    rT_t, rT = fcbuf("rT", BF16)
    kT_t, kT = fcbuf("kT", BF16)
    aT_t, aT = fcbuf("aT", BF16)
    bT_t, bT = fcbuf("bT", BF16)
    vb_t, vb = fcbuf("vb", BF16)
    bonus_t, bonus = fcbuf("bonus", BF16)
    bsum_t, bsum = fcbuf("bsum", BF16)
    gst_t, gst = fcbuf("gst", BF16)
    gC_t = sb("gC", [128, NFC, NJ], F32)
    gC = [rg(gC_t[:, fc, :], "gC%d" % fc) for fc in range(NFC)]
    khat_t = sb("khat", [128, NJ, FL], BF16)
    bhat_t = sb("bhat", [128, NJ, FL], BF16)
    Vtm_t = sb("Vtm", [128, NJ, FL], BF16)
    khat = [rg(khat_t[:, j, :], "khat%d" % j) for j in range(NJ)]
    bhat = [rg(bhat_t[:, j, :], "bhat%d" % j) for j in range(NJ)]
    Vtm = [rg(Vtm_t[:, j, :], "Vtm%d" % j) for j in range(NJ)]
    yout_t = sb("yout", [128, NFC, STW], BF16)
    yout = [rg(yout_t[:, :, j * CH:(j + 1) * CH], "yout%d" % j) for j in range(NJ)]

    NTMP = 7
    tmps = [rg(sb("tmp%d" % i, [128, STW], F32), "tmp%d" % i) for i in range(NTMP)]
    tctr = [0]

    def tmp():
        t = tmps[tctr[0] % NTMP]
        tctr[0] += 1
        return t

    t1b = rg(sb("t1b", [128, 2, STW], BF16), "t1b")

    Aseq_t = sb("Aseq", [128, 16, 512], BF16)
    Aseq = [rg(Aseq_t[:, h, :], "Aseq%d" % h) for h in range(16)]
    Pg = [rg(sb("Pg%d" % i, [128, 512], F32), "Pg%d" % i) for i in range(2)]
    PTg = [rg(sb("PTg%d" % i, [128, 512], F32), "PTg%d" % i) for i in range(2)]
    XTg = [rg(sb("XTg%d" % i, [128, 512], F32), "XTg%d" % i) for i in range(2)]
    Zb = rg(sb("Zb", [128, FL], BF16), "Zb")
    Ub = rg(sb("Ub", [128, FL], BF16), "Ub")
    ST32_t = sb("ST32", [128, NFC, 64], F32)
    ST32 = rg(ST32_t, "ST32")
    STpad_t = sb("STpad", [128, 16, 64], BF16)
    STe = rg(STpad_t[:, 0:16:2, :], "STe")
    STo = rg(STpad_t[:, 1:16:2, :], "STo")
    Ysb_t = sb("Ysb", [128, NFC, CH], F32)
    Ysb = rg(Ysb_t, "Ysb")
    Ysq_t = sb("Ysq", [128, NFC, CH], F32)
    Ysq = rg(Ysq_t, "Ysq")
    mean_t = sb("mean", [128, NFC, CH], F32)
    mean = rg(mean_t, "mean")
    var_t = sb("var", [128, NFC, CH], F32)
    var = rg(var_t, "var")

    ws = WStream(c, nslot=2)
    pp = PsumPool(c, 7)
    ptb_t = c.psum("ptb", [128, 1024], BF16)
    ptb = rg(ptb_t, "ptb")

    c.op("pool", "memset", ap=ST32[:], constant=0.0, writes=[ST32])
    c.op("pool", "memset", ap=STpad_t[:], constant=0.0, writes=[STe, STo])

    hv = hT.rearrange("(c p) t -> p c t", p=128)

    def proj_fc(w_d, cg, xs):
        slot = ws.load(w_d, 0, cg * 512)
        for fcl in range(4):
            fc = cg * 4 + fcl
            ps = pp.next()
            for k in range(16):
                c.op("pe", "matmul", out=ps[:, 0:STW], lhsT=slot[:, k, fcl * 128:(fcl + 1) * 128], rhs=xs[k][:],
                     start=(k == 0), stop=(k == 15), reads=[slot, xs[k]], writes=[ps])
            yield fc, ps

    for st in range(NST):
        t0 = st * STW
        c.dma("sp", hn_t[:, :, 1:STW + 1], hv[:, :, t0:t0 + STW], writes=hn)
        if st == 0:
            c.op("pool", "memset", ap=hn_t[:, :, 0:1], constant=0.0, writes=hn)
        else:
            c.op("pool", "tensor_copy", out=hn_t[:, :, 0:1], in_=carry[:], reads=[carry], writes=hn)
        sq = xbuf[0]
        for ch in range(16):
            c.op("act", "activation", out=sq[ch][:], in_=hn[ch][:, 1:STW + 1], func=AF.Square, reads=[hn[ch]], writes=[sq[ch]])
        ps = pp.next()
        for ch in range(16):
            c.op("pe", "matmul", out=ps[:, 0:STW], lhsT=onesb[:], rhs=sq[ch][:], start=(ch == 0), stop=(ch == 15),
                 reads=[onesb, sq[ch]], writes=[ps])
        tt = tmp()
        c.op("dve", "tensor_scalar", out=tt[:], in0=ps[:, 0:STW], scalar1=1.0 / D, scalar2=RMS_EPS, op0=ALU.mult, op1=ALU.add,
             reads=[ps], writes=[tt])
        c.op("act", "activation", out=tt[:], in_=tt[:], func=AF.Sqrt, reads=[tt], writes=[tt])
        c.op("dve", "reciprocal", out=rstd[:], in_=tt[:], reads=[tt], writes=[rstd])
        for ch in range(16):
            c.op("dve", "scalar_tensor_tensor", out=hn[ch][:, 1:STW + 1], in0=hn[ch][:, 1:STW + 1], scalar=gmix[:, ch:ch + 1],
                 in1=rstd[:], op0=ALU.mult, op1=ALU.mult, reads=[hn[ch], gmix, rstd], writes=[hn[ch]])
        c.op("pool", "tensor_copy", out=carry[:], in_=hn_t[:, :, STW:STW + 1], reads=hn, writes=[carry])

        xi = [0]

        def lerp(name):
            i = MU_ORDER[name]
            xs = xbuf[xi[0] % 2]
            xi[0] += 1
            for ch in range(16):
                tf = tmp()
                c.op("pool", "tensor_single_scalar", out=tf[:], in_=hn[ch][:, 1:STW + 1], scalar=omm[:, i * 16 + ch:i * 16 + ch + 1],
                     op=ALU.mult, reads=[hn[ch], omm], writes=[tf])
                c.op("dve", "scalar_tensor_tensor", out=xs[ch][:], in0=hn[ch][:, 0:STW], scalar=mu[:, i * 16 + ch:i * 16 + ch + 1],
                     in1=tf[:], op0=ALU.mult, op1=ALU.add, reads=[hn[ch], mu, tf], writes=[xs[ch]])
            return xs

        def lora1(w_d, ncols, xs, func, m=0):
            slot = ws.load(w_d, 0, m * 128, ncols=ncols)
            ps = pp.next()
            for k in range(16):
                c.op("pe", "matmul", out=ps[0:ncols, 0:STW], lhsT=slot[:, k, 0:ncols], rhs=xs[k][:], start=(k == 0), stop=(k == 15),
                     reads=[slot, xs[k]], writes=[ps])
            c.op("act", "activation", out=t1b[0:ncols, m, :], in_=ps[0:ncols, 0:STW], func=func, reads=[ps], writes=[t1b])

        if stop <= 0:
            break
        xs = lerp("w")
        lora1(w1_d, 96, xs, AF.Tanh)
        for fc in range(NFC):
            ps = pp.next()
            c.op("pe", "matmul", out=ps[:, 0:STW], lhsT=w2sb[0:96, fc * 128:(fc + 1) * 128], rhs=t1b[0:96, 0, :], start=True, stop=True,
                 reads=[w2sb, t1b], writes=[ps])
            lw = tmp()
            c.op("act", "activation", out=lw[:], in_=ps[:, 0:STW], func=AF.Sigmoid, bias=pcol(P_W0, fc), reads=[ps, pv], writes=[lw])
            c.op("dve", "tensor_single_scalar", out=lw[:], in_=lw[:], scalar=WSCALE, op=ALU.mult, reads=[lw], writes=[lw])
            cum = tmp()
            for j in range(NJ):
                c.op("dve", "tensor_tensor_scan", out=cum[:, j * CH:(j + 1) * CH], data0=onesf[:, 0:CH], data1=lw[:, j * CH:(j + 1) * CH],
                     initial=0.0, op0=ALU.mult, op1=ALU.add, reads=[onesf, lw], writes=[cum])
            ep = tmp()
            c.op("act", "activation", out=ep[:], in_=cum[:], func=AF.Exp, reads=[cum], writes=[ep])
            c.op("pool", "tensor_copy", out=Epos[fc][:], in_=ep[:], reads=[ep], writes=[Epos[fc]])
            c.op("dve", "tensor_copy", out=gC[fc][:], in_=ep[:, CH - 1:STW:CH], reads=[ep], writes=[gC[fc]])
            c.op("act", "activation", out=Eneg[fc][:], in_=cum[:], func=AF.Exp, scale=-1.0, reads=[cum], writes=[Eneg[fc]])
            cml = tmp()
            c.op("dve", "tensor_tensor", out=cml[:], in0=cum[:], in1=lw[:], op=ALU.subtract, reads=[cum, lw], writes=[cml])
            c.op("act", "activation", out=Eprev[fc][:], in_=cml[:], func=AF.Exp, reads=[cml], writes=[Eprev[fc]])

        if stop <= 1:
            break
        xs = lerp("a")
        lora1(a1_d, 96, xs, AF.Copy)
        for fc in range(NFC):
            ps = pp.next()
            c.op("pe", "matmul", out=ps[:, 0:STW], lhsT=a2sb[0:96, fc * 128:(fc + 1) * 128], rhs=t1b[0:96, 0, :], start=True, stop=True,
                 reads=[a2sb, t1b], writes=[ps])
            c.op("act", "activation", out=a_t[fc][:], in_=ps[:, 0:STW], func=AF.Sigmoid, bias=pcol(P_A0, fc), reads=[ps, pv], writes=[a_t[fc]])

        if stop <= 2:
            break
        xs = lerp("r")
        for cg in range(2):
            for fc, ps in proj_fc(wr_d, cg, xs):
                c.op("act", "activation", out=r_t[fc][:], in_=ps[:, 0:STW], func=AF.Copy, reads=[ps], writes=[r_t[fc]])
                c.op("dve", "tensor_tensor", out=rT[fc][:], in0=ps[:, 0:STW], in1=Epos[fc][:], op=ALU.mult, reads=[ps, Epos[fc]], writes=[rT[fc]])

        if stop <= 3:
            break
        xs = lerp("k")
        for cg in range(2):
            for fc, ps in proj_fc(wk_d, cg, xs):
                kkr = tmp()
                c.op("dve", "tensor_single_scalar", out=kkr[:], in_=ps[:, 0:STW], scalar=pcol(P_KK, fc), op=ALU.mult,
                     reads=[ps, pv], writes=[kkr])
                ksq = tmp()
                c.op("act", "activation", out=ksq[:], in_=kkr[:], func=AF.Square, reads=[kkr], writes=[ksq])
                ps2 = pp.next()
                c.op("pe", "matmul", out=ps2[:, 0:STW], lhsT=bones[:], rhs=ksq[:], start=True, stop=True, reads=[bones, ksq], writes=[ps2])
                rn = tmp()
                c.op("act", "activation", out=rn[:], in_=ps2[:, 0:STW], func=AF.Sqrt, reads=[ps2], writes=[rn])
                c.op("dve", "tensor_single_scalar", out=rn[:], in_=rn[:], scalar=1e-12, op=ALU.max, reads=[rn], writes=[rn])
                c.op("dve", "reciprocal", out=rn[:], in_=rn[:], reads=[rn], writes=[rn])
                kk = tmp()
                c.op("dve", "tensor_tensor", out=kk[:], in0=kkr[:], in1=rn[:], op=ALU.mult, reads=[kkr, rn], writes=[kk])
                fac = tmp()
                c.op("dve", "tensor_scalar", out=fac[:], in0=a_t[fc][:], scalar1=pcol(P_KA, fc), scalar2=pcol(P_OMKA, fc), op0=ALU.mult, op1=ALU.add,
                     reads=[a_t[fc], pv], writes=[fac])
                kmod = tmp()
                c.op("dve", "tensor_tensor", out=kmod[:], in0=ps[:, 0:STW], in1=fac[:], op=ALU.mult, reads=[ps, fac], writes=[kmod])
                rk = fac
                c.op("dve", "scalar_tensor_tensor", out=rk[:], in0=kmod[:], scalar=pcol(P_RK, fc), in1=r_t[fc][:], op0=ALU.mult, op1=ALU.mult,
                     reads=[kmod, pv, r_t[fc]], writes=[rk])
                ps3 = pp.next()
                c.op("pe", "matmul", out=ps3[:, 0:STW], lhsT=bones[:], rhs=rk[:], start=True, stop=True, reads=[bones, rk], writes=[ps3])
                c.op("act", "activation", out=bsum[fc][:], in_=ps3[:, 0:STW], func=AF.Copy, reads=[ps3], writes=[bsum[fc]])
                c.op("dve", "tensor_tensor", out=kT[fc][:], in0=kmod[:], in1=Eneg[fc][:], op=ALU.mult, reads=[kmod, Eneg[fc]], writes=[kT[fc]])
                c.op("dve", "scalar_tensor_tensor", out=aT[fc][:], in0=kk[:], scalar=-1.0, in1=Eprev[fc][:], op0=ALU.mult, op1=ALU.mult,
                     reads=[kk, Eprev[fc]], writes=[aT[fc]])
                bq = kkr
                c.op("dve", "tensor_tensor", out=bq[:], in0=kk[:], in1=a_t[fc][:], op=ALU.mult, reads=[kk, a_t[fc]], writes=[bq])
                c.op("dve", "tensor_tensor", out=bT[fc][:], in0=bq[:], in1=Eneg[fc][:], op=ALU.mult, reads=[bq, Eneg[fc]], writes=[bT[fc]])

        if stop <= 4:
            break
        xs = lerp("v")
        if layer == 1:
            lora1(v1_d, 64, xs, AF.Copy)
        for cg in range(2):
            for fc, ps in proj_fc(wv_d, cg, xs):
                vfin = tmp()
                if layer == 0:
                    c.op("act", "activation", out=vfin[:], in_=ps[:, 0:STW], func=AF.Copy, reads=[ps], writes=[vfin])
                    c.dma("act", vT_o[fc * 128:(fc + 1) * 128, t0:t0 + STW], vfin[:], reads=[vfin], is_out=True)
                else:
                    psv = pp.next()
                    c.op("pe", "matmul", out=psv[:, 0:STW], lhsT=v2sb[0:64, fc * 128:(fc + 1) * 128], rhs=t1b[0:64, 0, :], start=True, stop=True,
                         reads=[v2sb, t1b], writes=[psv])
                    sg = tmp()
                    c.op("act", "activation", out=sg[:], in_=psv[:, 0:STW], func=AF.Sigmoid, bias=pcol(P_V0, fc), reads=[psv, pv], writes=[sg])
                    vfc = tmp()
                    c.dma("sp", vfc[:], vfT[fc * 128:(fc + 1) * 128, t0:t0 + STW], writes=[vfc])
                    dd = tmp()
                    c.op("dve", "tensor_tensor", out=dd[:], in0=vfc[:], in1=ps[:, 0:STW], op=ALU.subtract, reads=[vfc, ps], writes=[dd])
                    c.op("dve", "tensor_tensor", out=dd[:], in0=dd[:], in1=sg[:], op=ALU.mult, reads=[dd, sg], writes=[dd])
                    c.op("dve", "tensor_tensor", out=vfin[:], in0=ps[:, 0:STW], in1=dd[:], op=ALU.add, reads=[ps, dd], writes=[vfin])
                c.op("dve", "tensor_tensor", out=bonus[fc][:], in0=vfin[:], in1=bsum[fc][:], op=ALU.mult, reads=[vfin, bsum[fc]], writes=[bonus[fc]])
                c.op("pool", "tensor_copy", out=vb[fc][:], in_=vfin[:], reads=[vfin], writes=[vb[fc]])

        if stop <= 5:
            break
        xs = lerp("g")
        for m in range(2):
            lora1(g1_d, 128, xs, AF.Sigmoid, m=m)
        for fc in range(NFC):
            ps = pp.next()
            for m in range(2):
                c.op("pe", "matmul", out=ps[:, 0:STW], lhsT=g2sb[:, m, fc * 128:(fc + 1) * 128], rhs=t1b[:, m, :], start=(m == 0), stop=(m == 1),
                     reads=[g2sb, t1b], writes=[ps])
            c.op("act", "activation", out=gst[fc][:], in_=ps[:, 0:STW], func=AF.Copy, reads=[ps], writes=[gst[fc]])

        if stop <= 6:
            break
        for src, dst in ((kT, khat), (bT, bhat), (vb, Vtm)):
            for j in range(NJ):
                for fc in range(NFC):
                    c.op("pe", "transpose", out=ptb[:, fc * 128:(fc + 1) * 128], in_=src[fc][:, j * CH:(j + 1) * CH], identity=ident[:],
                         reads=[src[fc], ident], writes=[ptb])
                c.op("act", "activation", out=dst[j][:], in_=ptb[:], func=AF.Copy, reads=[ptb], writes=[dst[j]])

        if stop <= 7:
            break
        for j in range(NJ):
            chs = slice(j * CH, (j + 1) * CH)
            for G in range(4):
                cur = 0
                for hl in range(4):
                    h = G * 4 + hl
                    fc = h // 2
                    hs = slice((h % 2) * 64, (h % 2) * 64 + 64)
                    pa = pp.next()
                    rd = [kT[fc], aT[fc], rT[fc], bT[fc]]
                    c.op("pe", "matmul", out=pa[:, 0:128], lhsT=kT[fc][hs, chs], rhs=aT[fc][hs, chs], start=True, stop=True, reads=rd, writes=[pa])
                    c.op("pe", "matmul", out=pa[:, 128:256], lhsT=kT[fc][hs, chs], rhs=rT[fc][hs, chs], start=True, stop=True, reads=rd, writes=[pa])
                    c.op("pe", "matmul", out=pa[:, 256:384], lhsT=bT[fc][hs, chs], rhs=rT[fc][hs, chs], start=True, stop=True, reads=rd, writes=[pa])
                    c.op("dve", "tensor_tensor", out=Aseq[h][:, 0:384], in0=pa[:, 0:384], in1=mask3[:], op=ALU.mult, reads=[pa, mask3], writes=[Aseq[h]])
                    cs = slice(hl * 128, (hl + 1) * 128)
                    psNT = pp.next()
                    c.op("pe", "matmul", out=psNT[:, 0:128], lhsT=bT[fc][hs, chs], rhs=aT[fc][hs, chs], start=True, stop=True, reads=rd, writes=[psNT])
                    c.op("dve", "tensor_tensor", out=PTg[cur][:, cs], in0=psNT[:, 0:128], in1=masksu[:, 0:128], op=ALU.mult, reads=[psNT, masksu], writes=[PTg[cur]])
                    psN = pp.next()
                    c.op("pe", "matmul", out=psN[:, 0:128], lhsT=aT[fc][hs, chs], rhs=bT[fc][hs, chs], start=True, stop=True, reads=rd, writes=[psN])
                    c.op("dve", "tensor_tensor", out=Pg[cur][:, cs], in0=psN[:, 0:128], in1=masksl[:, 0:128], op=ALU.mult, reads=[psN, masksl], writes=[Pg[cur]])
                c.op("dve", "tensor_tensor", out=XTg[cur][:].rearrange("p (h t) -> p h t", h=4), in0=PTg[cur][:].rearrange("p (h t) -> p h t", h=4),
                     in1=ident[:].unsqueeze(1).to_broadcast([128, 4, 128]), op=ALU.add, reads=[PTg[cur], ident], writes=[XTg[cur]])
                for lvl in range(6):
                    nxt = 1 - cur
                    pP = pp.next()
                    pPT = pp.next()
                    for hl in range(4):
                        cs = slice(hl * 128, (hl + 1) * 128)
                        c.op("pe", "matmul", out=pP[:, cs], lhsT=PTg[cur][:, cs], rhs=Pg[cur][:, cs], start=True, stop=True, reads=[PTg[cur], Pg[cur]], writes=[pP])
                        c.op("pe", "matmul", out=pPT[:, cs], lhsT=Pg[cur][:, cs], rhs=PTg[cur][:, cs], start=True, stop=True, reads=[PTg[cur], Pg[cur]], writes=[pPT])
                    c.op("act", "activation", out=Pg[nxt][:], in_=pP[:], func=AF.Copy, reads=[pP], writes=[Pg[nxt]])
                    c.op("act", "activation", out=PTg[nxt][:], in_=pPT[:], func=AF.Copy, reads=[pPT], writes=[PTg[nxt]])
                    pX = pp.next()
                    for hl in range(4):
                        cs = slice(hl * 128, (hl + 1) * 128)
                        c.op("pe", "matmul", out=pX[:, cs], lhsT=Pg[nxt][:, cs], rhs=XTg[cur][:, cs], start=True, stop=True, reads=[Pg[nxt], XTg[cur]], writes=[pX])
                    if lvl < 5:
                        c.op("dve", "tensor_tensor", out=XTg[nxt][:], in0=pX[:], in1=XTg[cur][:], op=ALU.add, reads=[pX, XTg[cur]], writes=[XTg[nxt]])
                    else:
                        c.op("dve", "tensor_tensor", out=Aseq_t[:, G * 4:(G + 1) * 4, 384:512], in0=pX[:].rearrange("p (h t) -> p h t", h=4),
                             in1=XTg[cur][:].rearrange("p (h t) -> p h t", h=4), op=ALU.add, reads=[pX, XTg[cur]], writes=Aseq[G * 4:(G + 1) * 4])
                    cur = nxt

            if stop <= 8:
                break
            def stp(h):
                return STe if h % 2 == 0 else STo

            pz = [pp.next(), pp.next()]
            for h in range(16):
                fc = h // 2
                o = pz[h // 8][:, (h % 8) * 64:(h % 8) * 64 + 64]
                c.op("pe", "matmul", out=o, lhsT=aT[fc][:, chs], rhs=STpad_t[:, h, :], start=True, stop=False, reads=[aT[fc], stp(h)], writes=[pz[h // 8]])
                c.op("pe", "matmul", out=o, lhsT=Aseq[h][:, 0:128], rhs=Vtm[j][:, h * 64:(h + 1) * 64], start=False, stop=True, reads=[Aseq[h], Vtm[j]], writes=[pz[h // 8]])
            for b in range(2):
                c.op("act", "activation", out=Zb[:, b * 512:(b + 1) * 512], in_=pz[b][:], func=AF.Copy, reads=[pz[b]], writes=[Zb])
            pu = [pp.next(), pp.next()]
            for h in range(16):
                o = pu[h // 8][:, (h % 8) * 64:(h % 8) * 64 + 64]
                c.op("pe", "matmul", out=o, lhsT=Aseq[h][:, 384:512], rhs=Zb[:, h * 64:(h + 1) * 64], start=True, stop=True, reads=[Aseq[h], Zb], writes=[pu[h // 8]])
            for b in range(2):
                c.op("dve", "tensor_copy", out=Ub[:, b * 512:(b + 1) * 512], in_=pu[b][:], reads=[pu[b]], writes=[Ub])
            py = [pp.next(), pp.next()]
            for h in range(16):
                fc = h // 2
                hs = slice((h % 2) * 64, (h % 2) * 64 + 64)
                o = py[fc // 4][hs, (fc % 4) * 128:(fc % 4) * 128 + 128]
                c.op("pe", "matmul", out=o, lhsT=STpad_t[:, h, :], rhs=rT[fc][:, chs], start=True, stop=False, reads=[stp(h), rT[fc]], writes=[py[fc // 4]])
                c.op("pe", "matmul", out=o, lhsT=Vtm[j][:, h * 64:(h + 1) * 64], rhs=Aseq[h][:, 128:256], start=False, stop=False, reads=[Vtm[j], Aseq[h]], writes=[py[fc // 4]])
                c.op("pe", "matmul", out=o, lhsT=Ub[:, h * 64:(h + 1) * 64], rhs=Aseq[h][:, 256:384], start=False, stop=True, reads=[Ub, Aseq[h]], writes=[py[fc // 4]])
            pst = pp.next()
            for h in range(16):
                fc = h // 2
                hs = slice((h % 2) * 64, (h % 2) * 64 + 64)
                o = pst[hs, fc * 64:(fc + 1) * 64]
                c.op("pe", "matmul", out=o, lhsT=khat[j][:, h * 64:(h + 1) * 64], rhs=Vtm[j][:, h * 64:(h + 1) * 64], start=True, stop=False, reads=[khat[j], Vtm[j]], writes=[pst])
                c.op("pe", "matmul", out=o, lhsT=bhat[j][:, h * 64:(h + 1) * 64], rhs=Ub[:, h * 64:(h + 1) * 64], start=False, stop=True, reads=[bhat[j], Ub], writes=[pst])
            c.op("dve", "tensor_tensor", out=ST32[:], in0=pst[:].rearrange("p (f i) -> p f i", f=NFC), in1=ST32[:], op=ALU.add, reads=[pst, ST32], writes=[ST32])
            c.op("dve", "tensor_tensor", out=ST32[:], in0=ST32[:], in1=gC_t[:, :, j:j + 1].to_broadcast([128, NFC, 64]), op=ALU.mult, reads=[ST32] + gC, writes=[ST32])
            c.op("pool", "tensor_copy", out=STpad_t[0:64, 0:16:2, :], in_=ST32_t[0:64, :, :], reads=[ST32], writes=[STe])
            c.op("pool", "tensor_copy", out=STpad_t[64:128, 1:16:2, :], in_=ST32_t[64:128, :, :], reads=[ST32], writes=[STo])
            if stop <= 9:
                break
            for b in range(2):
                c.op("act", "activation", out=Ysb_t[:, b * 4:(b + 1) * 4, :], in_=py[b][:].rearrange("p (f t) -> p f t", f=4), func=AF.Copy, reads=[py[b]], writes=[Ysb])
            c.op("pool", "tensor_tensor", out=Ysq[:], in0=Ysb[:], in1=Ysb[:], op=ALU.mult, reads=[Ysb], writes=[Ysq])
            for b in range(2):
                pm = pp.next()
                pq = pp.next()
                c.op("pe", "matmul", out=pm[:], lhsT=bones[:], rhs=Ysb_t[:, b * 4:(b + 1) * 4, :], start=True, stop=True, reads=[bones, Ysb], writes=[pm])
                c.op("pe", "matmul", out=pq[:], lhsT=bones[:], rhs=Ysq_t[:, b * 4:(b + 1) * 4, :], start=True, stop=True, reads=[bones, Ysq], writes=[pq])
                mv = mean_t[:, b * 4:(b + 1) * 4, :]
                vv = var_t[:, b * 4:(b + 1) * 4, :]
                c.op("act", "activation", out=mv, in_=pm[:].rearrange("p (f t) -> p f t", f=4), func=AF.Copy, scale=1.0 / 64, reads=[pm], writes=[mean])
                c.op("dve", "tensor_tensor", out=vv, in0=mv, in1=mv, op=ALU.mult, reads=[mean], writes=[var])
                c.op("dve", "scalar_tensor_tensor", out=vv, in0=pq[:].rearrange("p (f t) -> p f t", f=4), scalar=1.0 / 64, in1=vv, op0=ALU.mult, op1=ALU.subtract,
                     reads=[pq, var], writes=[var])
            c.op("dve", "tensor_single_scalar", out=var[:], in_=var[:], scalar=GN_EPS, op=ALU.add, reads=[var], writes=[var])
            c.op("act", "activation", out=var[:], in_=var[:], func=AF.Sqrt, reads=[var], writes=[var])
            c.op("dve", "reciprocal", out=var[:], in_=var[:], reads=[var], writes=[var])
            c.op("dve", "tensor_tensor", out=Ysb[:], in0=Ysb[:], in1=mean[:], op=ALU.subtract, reads=[Ysb, mean], writes=[Ysb])
            c.op("dve", "tensor_tensor", out=Ysb[:], in0=Ysb[:], in1=var[:], op=ALU.mult, reads=[Ysb, var], writes=[Ysb])
            gw = pv[:, P_GW * 8:(P_GW + 1) * 8].unsqueeze(2).to_broadcast([128, NFC, CH])
            gb = pv[:, P_GB * 8:(P_GB + 1) * 8].unsqueeze(2).to_broadcast([128, NFC, CH])
            c.op("dve", "tensor_tensor", out=Ysb[:], in0=Ysb[:], in1=gw, op=ALU.mult, reads=[Ysb, pv], writes=[Ysb])
            c.op("dve", "tensor_tensor", out=Ysb[:], in0=Ysb[:], in1=gb, op=ALU.add, reads=[Ysb, pv], writes=[Ysb])
            c.op("dve", "tensor_tensor", out=Ysb[:], in0=Ysb[:], in1=bonus_t[:, :, chs], op=ALU.add, reads=[Ysb] + bonus, writes=[Ysb])
            c.op("dve", "tensor_tensor", out=yout[j][:], in0=Ysb[:], in1=gst_t[:, :, chs], op=ALU.mult, reads=[Ysb] + gst, writes=[yout[j]])
        if stop <= 9:
            break
        c.dma("sp", ygT.rearrange("(c p) t -> p c t", p=128)[:, :, t0:t0 + STW], yout_t[:], reads=yout, is_out=True)

    c.wait_all_outputs("sp")
    c.finalize()
    return nc


HD = 128
NQH = 16
NKV = 4
BS = 256
NBLK = 8
TFULL = 2048
NEG = -30000.0
SCALE = HD ** -0.5
NKB = [5, 6, 7, 8]


def _rope(c, ps, cosF, sinS, ts, tA, tB, out_ap, out_R):
    n = ts.stop - ts.start
    c.op("dve", "tensor_tensor", out=tA[:, 0:n], in0=ps[:, 0:n], in1=cosF[:, ts], op=ALU.mult, reads=[ps, cosF], writes=[tA])
    c.op("dve", "tensor_tensor", out=tB[0:64, 0:n], in0=ps[64:128, 0:n], in1=sinS[0:64, ts], op=ALU.mult, reads=[ps, sinS], writes=[tB])
    c.op("dve", "tensor_tensor", out=tB[64:128, 0:n], in0=ps[0:64, 0:n], in1=sinS[64:128, ts], op=ALU.mult, reads=[ps, sinS], writes=[tB])
    c.op("dve", "tensor_tensor", out=out_ap, in0=tA[:, 0:n], in1=tB[:, 0:n], op=ALU.add, reads=[tA, tB], writes=[out_R])


def _load_resid_and_norm(c, nc, pp, hT, g_d, ones_d):
    resid_t = c.sbuf("resid", [128, NC16, NT], F32)
    resid = [c.reg(resid_t[:, ch, :], "resid%d" % ch) for ch in range(NC16)]
    xn_t = c.sbuf("xn", [128, NC16, NT], BF16)
    xn = [[c.reg(xn_t[:, ch, tt * TT:(tt + 1) * TT], "xn") for tt in range(NTT)] for ch in range(NC16)]
    sq_t = c.sbuf("sqs", [128, NC16, NT], BF16)
    sq = [[c.reg(sq_t[:, ch, tt * TT:(tt + 1) * TT], "sq") for tt in range(NTT)] for ch in range(NC16)]
    gcol = c.reg(c.sbuf("gcol", [128, 16], F32), "gcol")
    ones = c.reg(c.sbuf("ones_sb", [128, 128], BF16), "ones")
    rstd = [c.reg(c.sbuf("rstd%d" % tt, [128, TT], F32), "rstd") for tt in range(NTT)]
    ntmp = c.reg(c.sbuf("ntmp", [128, TT], F32), "ntmp")
    c.dma("sp", gcol[:], g_d, writes=[gcol])
    c.dma("sp", ones[:], ones_d, writes=[ones])
    hv = hT.rearrange("(c p) t -> p c t", p=128)
    for ch in range(NC16):
        c.dma("sp" if ch % 2 == 0 else "act", resid[ch][:], hv[:, ch, :], writes=[resid[ch]])
    emit_rmsnorm(c, pp, resid, sq, ones, gcol, rstd, xn, ntmp)
    return xn, ones, sq_t, sq


def build_kv():
    nc = bass.Bass("TRN2", target_bir_lowering=False)

    def din(name, shape, dt=F32):
        return nc.dram_tensor(name, list(shape), dt, kind="ExternalInput").ap()

    hT = din("hT", [D, NT])
    g_d = din("g", [128, 16])
    wkv = din("wkv", [D, 1024])
    ones_d = din("ones", [128, 128], BF16)
    cos_d = din("cosF", [128, NT])
    sin_d = din("sinS", [128, NT])
    kT_o = nc.dram_tensor("kT", [NKV * HD, NT], BF16, kind="ExternalOutput").ap()
    v_o = nc.dram_tensor("v", [NT, NKV * HD], BF16, kind="ExternalOutput").ap()

    c = Ctx(nc)
    pp = PsumPool(c, 8)
    ws = WStream(c, nslot=2)
    xn, ones, sq_t, sq = _load_resid_and_norm(c, nc, pp, hT, g_d, ones_d)
    cosF = c.reg(c.sbuf("cosF_sb", [128, NT], F32), "cosF")
    sinS = c.reg(c.sbuf("sinS_sb", [128, NT], F32), "sinS")
    c.dma("act", cosF[:], cos_d, writes=[cosF])
    c.dma("act", sinS[:], sin_d, writes=[sinS])
    tA = c.reg(c.sbuf("tA", [128, TT], F32), "tA")
    tB = c.reg(c.sbuf("tB", [128, TT], F32), "tB")
    kout_t = c.sbuf("kout", [128, NKV, NT], BF16)
    kout = [c.reg(kout_t[:, g, :], "kout%d" % g) for g in range(NKV)]
    vout_t = c.sbuf("vout", [128, NT // 128, NKV * HD], BF16)
    vout = [c.reg(vout_t[:, i, :], "vout%d" % i) for i in range(NT // 128)]

    slot = ws.load(wkv, 0, 0)
    for g in range(NKV):
        for tt in range(NTT):
            ts = slice(tt * TT, (tt + 1) * TT)
            ps = pp.next()
            for k in range(NC16):
                c.op("pe", "matmul", out=ps[:], lhsT=slot[:, k, g * 128:(g + 1) * 128], rhs=xn[k][tt][:], start=(k == 0), stop=(k == NC16 - 1),
                     reads=[slot, xn[k][tt]], writes=[ps])
            _rope(c, ps, cosF, sinS, ts, tA, tB, kout[g][:, ts], kout[g])
    slot = ws.load(wkv, 0, 512)
    for i in range(NT // 128):
        tt, off = divmod(i * 128, TT)
        ps = pp.next()
        for k in range(NC16):
            c.op("pe", "matmul", out=ps[:], lhsT=xn[k][tt][:, off:off + 128], rhs=slot[:, k, :], start=(k == 0), stop=(k == NC16 - 1),
                 reads=[slot, xn[k][tt]], writes=[ps])
        c.op("act", "activation", out=vout[i][:], in_=ps[:], func=AF.Copy, reads=[ps], writes=[vout[i]])
    c.dma("sp", kT_o.rearrange("(g p) t -> p g t", p=128), kout_t[:], reads=kout, is_out=True)
    c.dma("act", v_o.rearrange("(i p) f -> p i f", p=128), vout_t[:], reads=vout, is_out=True)
    c.wait_all_outputs("sp")
    c.finalize()
    return nc


def build_mb():
    nc = bass.Bass("TRN2", target_bir_lowering=False)

    def din(name, shape, dt=F32):
        return nc.dram_tensor(name, list(shape), dt, kind="ExternalInput").ap()

    hT = din("hT", [D, NT])
    g_d = din("g", [128, 16])
    wq = din("wq", [D, D])
    ones_d = din("ones", [128, 128], BF16)
    ident_d = din("ident", [128, 128], BF16)
    cos_d = din("cosF", [128, NT])
    sin_d = din("sinS", [128, NT])
    kT_d = din("kTf", [NKV * HD, TFULL], BF16)
    v_d = din("vf", [TFULL, NKV * HD], BF16)
    negbig_d = din("negbig", [128, 4 * 128])
    vmul_d = din("vmul", [128, 4 * 128])
    cmask_d = din("cmask", [128, 4 * 8 * 2 * BS], BF16)
    oT = nc.dram_tensor("aT", [D, NT], BF16, kind="ExternalOutput").ap()

    c = Ctx(nc)
    pp = PsumPool(c, 3)
    accO = [c.reg(c.psum("accO%d" % i, [128, 512]), "accO%d" % i) for i in range(2)]
    accD = [c.reg(c.psum("accD%d" % i, [128, 512]), "accD%d" % i) for i in range(2)]
    ptb = c.reg(c.psum("ptb", [128, 1024], BF16), "ptb")
    ws = WStream(c, nslot=2)

    resid_t = c.sbuf("resid", [128, NC16, NT], F32)
    resid = [c.reg(resid_t[:, ch, :], "resid%d" % ch) for ch in range(NC16)]
    xn_t = c.sbuf("xn", [128, NC16, NT], BF16)
    xn = [[c.reg(xn_t[:, ch, tt * TT:(tt + 1) * TT], "xn") for tt in range(NTT)] for ch in range(NC16)]
    sq_t = c.sbuf("sqs", [128, NC16, NT], BF16)
    sq = [[c.reg(sq_t[:, ch, tt * TT:(tt + 1) * TT], "sq") for tt in range(NTT)] for ch in range(NC16)]
    gcol = c.reg(c.sbuf("gcol", [128, 16], F32), "gcol")
    ones = c.reg(c.sbuf("ones_sb", [128, 128], BF16), "ones")
    rstd = [c.reg(c.sbuf("rstd%d" % tt, [128, TT], F32), "rstd") for tt in range(NTT)]
    ntmp = c.reg(c.sbuf("ntmp", [128, TT], F32), "ntmp")
    c.dma("sp", gcol[:], g_d, writes=[gcol])
    c.dma("sp", ones[:], ones_d, writes=[ones])
    hv = hT.rearrange("(c p) t -> p c t", p=128)
    for ch in range(NC16):
        c.dma("sp" if ch % 2 == 0 else "act", resid[ch][:], hv[:, ch, :], writes=[resid[ch]])
    emit_rmsnorm(c, pp, resid, sq, ones, gcol, rstd, xn, ntmp)

    ident = c.reg(c.sbuf("ident_sb", [128, 128], BF16), "ident")
    c.dma("sp", ident[:], ident_d, writes=[ident])
    cosF = c.reg(c.sbuf("cosF_sb", [128, NT], F32), "cosF")
    sinS = c.reg(c.sbuf("sinS_sb", [128, NT], F32), "sinS")
    c.dma("act", cosF[:], cos_d, writes=[cosF])
    c.dma("act", sinS[:], sin_d, writes=[sinS])
    tA_t = c.sbuf("tA", [128, TT], F32)
    tB_t = c.sbuf("tB", [128, TT], F32)
    tA = c.reg(tA_t, "tA")
    tB = c.reg(tB_t, "tB")
    negbig = c.reg(c.sbuf("negbig_sb", [128, 512], F32), "negbig")
    vmul = c.reg(c.sbuf("vmul_sb", [128, 512], F32), "vmul")
    c.dma("sp", negbig[:], negbig_d, writes=[negbig])
    c.dma("sp", vmul[:], vmul_d, writes=[vmul])

    sq_all = [r for row in sq for r in row]
    kT_t = resid_t[:, 0:4, :].bitcast(BF16)
    V_t = resid_t[:, 4:8, :].rearrange("p c t -> p (c t)").bitcast(BF16).rearrange("p (i f) -> p i f", f=NKV * HD)
    kTs = [c.reg(kT_t[:, g, :], "kTs%d" % g) for g in range(NKV)]
    Vs = [c.reg(V_t[:, i, :], "Vs%d" % i) for i in range(TFULL // 128)]
    c.alias(kTs, resid[0:4])
    c.alias(Vs, resid[4:8])
    c.dma("sp", kT_t, kT_d.rearrange("(g p) t -> p g t", p=128), writes=kTs)
    c.dma("act", V_t, v_d.rearrange("(i p) f -> p i f", p=128), writes=Vs)

    qT_t = sq_t
    qT = [c.reg(qT_t[:, h, :], "qT%d" % h) for h in range(NQH)]
    for h in range(NQH):
        c.alias([qT[h]], sq[h])
    for hg in range(4):
        slot = ws.load(wq, 0, hg * 512)
        for hl in range(4):
            h = hg * 4 + hl
            for tt in range(NTT):
                ts = slice(tt * TT, (tt + 1) * TT)
                ps = pp.next()
                for k in range(NC16):
                    c.op("pe", "matmul", out=ps[:], lhsT=slot[:, k, hl * 128:(hl + 1) * 128], rhs=xn[k][tt][:], start=(k == 0), stop=(k == NC16 - 1),
                         reads=[slot, xn[k][tt]], writes=[ps])
                _rope(c, ps, cosF, sinS, ts, tA, tB, qT[h][:, ts], qT[h])

    ao_t = xn_t
    ao = [c.reg(ao_t[:, h, :], "ao%d" % h) for h in range(NQH)]
    for h in range(NQH):
        c.alias([ao[h]], xn[h])

    kmf = c.reg(c.sbuf("kmf", [128, NKV, NBLK], F32), "kmf")
    kmb = c.reg(c.sbuf("kmb", [128, NKV, NBLK], BF16), "kmb")
    for g in range(NKV):
        c.op("dve", "tensor_reduce", out=kmf[:, g, :], in_=kT_t[:, g, :].rearrange("p (n s) -> p n s", s=BS), axis=AX.X, op=ALU.add,
             reads=[kTs[g]], writes=[kmf])
    c.op("dve", "tensor_single_scalar", out=kmb[:], in_=kmf[:], scalar=1.0 / BS, op=ALU.mult, reads=[kmf], writes=[kmb])

    biasT_t = c.sbuf("biasT", [128, NT], BF16)
    biasT = [c.reg(biasT_t[:, i * BS:(i + 1) * BS], "biasT%d" % i) for i in range(4)]
    Gs = c.reg(c.sbuf("Gs", [128, 128], F32), "Gs")
    m8 = c.reg(c.sbuf("m8", [128, 128], F32), "m8")
    lt = c.reg(c.sbuf("lt", [128, 128], F32), "lt")
    bb = c.reg(c.sbuf("bb", [128, 128], BF16), "bb")
    for qt in range(NT // 128):
        i = qt // 2
        tsl = slice(qt * 128, (qt + 1) * 128)
        pg = pp.next()
        for h in range(NQH):
            c.op("pe", "matmul", out=pg[:, h * 8:(h + 1) * 8], lhsT=qT[h][:, tsl], rhs=kmb[:, h // 4, :], start=True, stop=True,
                 reads=[qT[h], kmb], writes=[pg])
        c.op("dve", "tensor_tensor", out=Gs[:], in0=pg[:, 0:128], in1=negbig[:, i * 128:(i + 1) * 128], op=ALU.add, reads=[pg, negbig], writes=[Gs])
        for h in range(NQH):
            c.op("dve", "max", out=m8[:, h * 8:(h + 1) * 8], in_=Gs[:, h * 8:(h + 1) * 8], reads=[Gs], writes=[m8])
        c.op("dve", "tensor_tensor", out=lt[:].rearrange("p (h n) -> p h n", n=8), in0=Gs[:].rearrange("p (h n) -> p h n", n=8),
             in1=m8[:].rearrange("p (h n) -> p h n", n=8)[:, :, 2:3].to_broadcast([128, NQH, 8]), op=ALU.is_lt, reads=[Gs, m8], writes=[lt])
        c.op("dve", "tensor_tensor", out=bb[:], in0=lt[:], in1=vmul[:, i * 128:(i + 1) * 128], op=ALU.mult, reads=[lt, vmul], writes=[bb])
        c.op("pe", "transpose", out=ptb[:, 0:128], in_=bb[:], identity=ident[:], reads=[bb, ident], writes=[ptb])
        c.op("act", "activation", out=biasT_t[:, tsl], in_=ptb[:, 0:128], func=AF.Copy, reads=[ptb], writes=[biasT[i]])

    cmv = cmask_d.rearrange("p (i j k t) -> p i j k t", i=4, j=8, k=2)
    plan = {}
    for i in range(4):
        for j in range(NKB[i]):
            qbs = (i, 4 + i)
            need_gate = any(j < qb for qb in qbs)
            need_const = any(j >= qb for qb in qbs)
            cmr = None
            if need_const:
                cmr = c.reg(c.sbuf("cm_%d_%d" % (i, j), [128, 2, BS], BF16), "cm_%d_%d" % (i, j))
                c.dma("sp", cmr[:], cmv[:, i, j, :, :], writes=[cmr])
            plan[(i, j)] = (need_gate, cmr)

    tAb = tA_t[:, :].bitcast(BF16)
    Pt = [c.reg(tAb[:, k * BS:(k + 1) * BS], "Pt%d" % k) for k in range(3)]
    rden = c.reg(tB_t[:, 0:BS], "rden")
    c.alias(Pt, [tA])
    c.alias([rden], [tB])
    pk = 0
    it = 0
    for i in range(4):
        qs = slice(i * BS, (i + 1) * BS)
        for h in range(NQH):
            g = h // 4
            aO, aD = accO[it % 2], accD[it % 2]
            it += 1
            ntile = NKB[i] * 2
            n = 0
            for j in range(NKB[i]):
                need_gate, cmr = plan[(i, j)]
                for kt in range(2):
                    ks = slice(j * BS + kt * 128, j * BS + kt * 128 + 128)
                    st_ = pp.next()
                    last_mm = "main"
                    if cmr is not None:
                        last_mm = "const"
                    elif need_gate:
                        last_mm = "gate"
                    c.op("pe", "matmul", out=st_[:, 0:BS], lhsT=kTs[g][:, ks], rhs=qT[h][:, qs], start=True, stop=(last_mm == "main"), reads=[kTs[g], qT[h]], writes=[st_])
                    if need_gate:
                        col = h * 8 + j
                        c.op("pe", "matmul", out=st_[:, 0:BS], lhsT=ident[:, col:col + 1].to_broadcast([128, 128]), rhs=biasT_t[:, qs], start=False, stop=(last_mm == "gate"),
                             reads=[ident, biasT[i]], writes=[st_])
                    if cmr is not None:
                        c.op("pe", "matmul", out=st_[:, 0:BS], lhsT=ident[:], rhs=cmr[:, kt, :], start=False, stop=True, reads=[ident, cmr], writes=[st_])
                    p_ = Pt[pk % 3]
                    pk += 1
                    c.op("act", "activation", out=p_[:], in_=st_[:, 0:BS], func=AF.Exp, scale=SCALE, reads=[st_], writes=[p_])
                    vi = (j * BS + kt * 128) // 128
                    c.op("pe", "matmul", out=aO[:, 0:BS], lhsT=V_t[:, vi, g * 128:(g + 1) * 128], rhs=p_[:], start=(n == 0), stop=(n == ntile - 1), reads=[Vs[vi], p_], writes=[aO])
                    c.op("pe", "matmul", out=aD[:, 0:BS], lhsT=ones[:], rhs=p_[:], start=(n == 0), stop=(n == ntile - 1), reads=[ones, p_], writes=[aD])
                    n += 1
            c.op("dve", "reciprocal", out=rden[:], in_=aD[:, 0:BS], reads=[aD], writes=[rden])
            c.op("dve", "tensor_tensor", out=ao[h][:, qs], in0=aO[:, 0:BS], in1=rden[:], op=ALU.mult, reads=[aO, rden], writes=[ao[h]])
    c.dma("sp", oT.rearrange("(h p) t -> p h t", p=128), ao_t[:], reads=ao, is_out=True)
    c.wait_all_outputs("sp")
    c.finalize()
    return nc


import ml_dtypes
from concourse.bass_utils import run_bass_kernel_spmd

_BF = ml_dtypes.bfloat16
_NCORES = 8
_PROGS = {}


def _prog(key, fn):
    if key not in _PROGS:
        _PROGS[key] = fn()
    return _PROGS[key]


def _lay(vec, n):
    return np.ascontiguousarray(np.asarray(vec, np.float32).reshape(n, 128).T)


def _consts_rw():
    su = np.triu(np.ones((128, 128), np.float32), 1)
    ui = np.triu(np.ones((128, 128), np.float32), 0)
    sl_ = np.tril(np.ones((128, 128), np.float32), -1)
    bones = np.zeros((128, 128), np.float32)
    bones[:64, :64] = 1
    bones[64:, 64:] = 1
    return {"ident": np.eye(128, dtype=np.float32).astype(_BF), "onesb": np.ones((128, 128), _BF), "bones": bones,
            "onesf": np.ones((128, 128), np.float32), "mask3": np.concatenate([su, ui, ui], 1).astype(_BF),
            "masksu": su.astype(_BF), "masksl": sl_.astype(_BF)}


def _rope_tables(pos):
    inv = 10000.0 ** (-np.arange(64, dtype=np.float64) / 64)
    ang = pos[None, :].astype(np.float64) * inv[:, None]
    cosF = np.concatenate([np.cos(ang), np.cos(ang)], 0).astype(np.float32)
    sinS = np.concatenate([-np.sin(ang), np.sin(ang)], 0).astype(np.float32)
    return cosF, sinS


def _mb_tables(half):
    negbig = np.zeros((4, NQH, 8), np.float32)
    vmul = np.zeros((4, NQH, 8), np.float32)
    cmask = np.zeros((128, 4, 8, 2, BS), np.float32)
    for i in range(4):
        qb = half * 4 + i
        negbig[i, :, qb:] = -1e30
        vmul[i, :, :qb] = NEG
        for j in range(8):
            for kt in range(2):
                s_glob = j * BS + kt * 128 + np.arange(128)[:, None]
                t_glob = qb * BS + np.arange(BS)[None, :]
                if j == qb:
                    cmask[:, i, j, kt] = np.where(s_glob <= t_glob, 0.0, NEG)
                elif j > qb:
                    cmask[:, i, j, kt] = NEG
    return (np.tile(negbig.reshape(1, 512), (128, 1)), np.tile(vmul.reshape(1, 512), (128, 1)), cmask.reshape(128, -1).astype(_BF))


def _launch(nc, in_maps):
    res = run_bass_kernel_spmd(nc, in_maps, core_ids=list(range(_NCORES)))
    return res.results


def kernel(x, ln_mix_g, ln_ffn_g, w_ff1, w_ff2, rw_mu, rw_w_rkv, rw_w0, rw_w1, rw_w2, rw_a0, rw_a1, rw_a2,
           rw_g1, rw_g2, rw_k_k, rw_k_a, rw_r_k, rw_gn_w, rw_gn_b, rw_w_o, rw_v0, rw_v1, rw_v2,
           kv_norm_g, w_kv, mb_w_q, mb_w_o, final_g):
    f32 = np.float32
    A = lambda a: np.ascontiguousarray(np.asarray(a, f32))
    x = A(x)
    B, T, Dm = x.shape
    hT = [np.ascontiguousarray(x[b].T) for b in range(B)]
    ones_bf = np.ones((128, 128), _BF)
    ident_bf = np.eye(128, dtype=f32).astype(_BF)
    crw = _consts_rw()
    vT_first = None

    def run_ff(layer, mT_cores, w_o):
        final = (layer == 3)
        nc = _prog(("ff", final), lambda: build_ff(final=final))
        g = np.concatenate([_lay(ln_ffn_g[layer], 16), _lay(final_g, 16)], 1)
        w1, w2, wo = A(w_ff1[layer]), A(w_ff2[layer]), A(w_o)
        maps = []
        for c in range(_NCORES):
            b, s = divmod(c, 2)
            maps.append({"hT": np.ascontiguousarray(hT[b][:, s * NT:(s + 1) * NT]), "mT": mT_cores[c], "wo": wo, "w1": w1, "w2": w2,
                         "g": g, "ones": ones_bf})
        out = _launch(nc, maps)
        for b in range(B):
            hT[b] = np.ascontiguousarray(np.concatenate([out[2 * b]["oT"], out[2 * b + 1]["oT"]], 1))

    for i in range(2):
        nc = _prog(("rw", i), lambda: build_rw(i, T))
        maps = []
        for c in range(_NCORES):
            b, s = divmod(c, 2)
            sl = slice(s * FL, (s + 1) * FL)
            rk = np.asarray(rw_r_k[i], f32).reshape(-1)
            v0 = np.asarray(rw_v0[0], f32) if i == 1 else np.zeros(Dm, f32)
            pvv = np.concatenate([_lay(rw_w0[i][sl], 8), _lay(rw_a0[i][sl], 8), _lay(v0[sl], 8), _lay(rw_k_k[i][sl], 8), _lay(rw_k_a[i][sl], 8),
                                  _lay(rk[sl], 8), _lay(rw_gn_w[i][sl], 8), _lay(rw_gn_b[i][sl], 8),
                                  np.zeros((128, 8), np.float32)], 1)
            d = {"hT": hT[b], "gmix": _lay(ln_mix_g[i], 16), "mu": np.concatenate([_lay(rw_mu[i][k], 16) for k in range(6)], 1),
                 "wr": A(rw_w_rkv[i][0][:, sl]), "wk": A(rw_w_rkv[i][1][:, sl]), "wv": A(rw_w_rkv[i][2][:, sl]),
                 "w1": A(rw_w1[i]), "a1": A(rw_a1[i]), "g1": A(rw_g1[i]),
                 "w2": A(rw_w2[i][:, sl]), "a2": A(rw_a2[i][:, sl]), "g2": A(rw_g2[i][:, sl]), "pv": pvv}
            if i == 1:
                d.update({"v1": A(rw_v1[0]), "v2": A(rw_v2[0][:, sl]), "vfT": vT_first[c]})
            d.update(crw)
            maps.append(d)
        out = _launch(nc, maps)
        if i == 0:
            vT_first = [out[c]["vT"] for c in range(_NCORES)]
        mT = []
        for c in range(_NCORES):
            b, s = divmod(c, 2)
            full = np.concatenate([out[2 * b]["ygT"], out[2 * b + 1]["ygT"]], 0)
            mT.append(np.ascontiguousarray(full[:, s * NT:(s + 1) * NT]))
        run_ff(i, mT, rw_w_o[i])

    nc = _prog(("kv",), build_kv)
    maps = []
    for c in range(_NCORES):
        b, s = divmod(c, 2)
        cosF, sinS = _rope_tables(np.arange(s * NT, (s + 1) * NT))
        maps.append({"hT": np.ascontiguousarray(hT[b][:, s * NT:(s + 1) * NT]), "g": _lay(kv_norm_g, 16), "wkv": A(w_kv), "ones": ones_bf,
                     "cosF": cosF, "sinS": sinS})
    out = _launch(nc, maps)
    kTf = [np.ascontiguousarray(np.concatenate([out[2 * b]["kT"], out[2 * b + 1]["kT"]], 1)) for b in range(B)]
    vf = [np.ascontiguousarray(np.concatenate([out[2 * b]["v"], out[2 * b + 1]["v"]], 0)) for b in range(B)]

    for j in range(2):
        layer = 2 + j
        nc = _prog(("mb",), build_mb)
        maps = []
        for c in range(_NCORES):
            b, s = divmod(c, 2)
            cosF, sinS = _rope_tables(np.arange(s * NT, (s + 1) * NT))
            negbig, vmul, cmask = _mb_tables(s)
            maps.append({"hT": np.ascontiguousarray(hT[b][:, s * NT:(s + 1) * NT]), "g": _lay(ln_mix_g[layer], 16), "wq": A(mb_w_q[j]),
                         "ones": ones_bf, "ident": ident_bf, "cosF": cosF, "sinS": sinS, "kTf": kTf[b], "vf": vf[b],
                         "negbig": negbig, "vmul": vmul, "cmask": cmask})
        out = _launch(nc, maps)
        run_ff(layer, [out[c]["aT"] for c in range(_NCORES)], mb_w_o[j])

    return np.ascontiguousarray(np.stack([hT[b].T for b in range(B)], 0)).astype(np.float32)
```

```python
from contextlib import ExitStack
import numpy as np
import concourse.bass as bass
import concourse.mybir as mybir

F32 = mybir.dt.float32
BF16 = mybir.dt.bfloat16
ALU = mybir.AluOpType
AF = mybir.ActivationFunctionType
AX = mybir.AxisListType

ENGS = ("pe", "act", "dve", "pool", "sp")


class R:
    __slots__ = ("ap", "name", "last_w", "reads", "dsem", "dcnt", "is_psum")

    def __init__(self, ap, name=""):
        self.ap = ap
        self.name = name
        self.last_w = None
        self.reads = []
        self.dsem = None
        self.dcnt = 0
        self.is_psum = False

    def __getitem__(self, idx):
        return self.ap[idx]


class Ctx:
    def __init__(self, nc):
        self.nc = nc
        self.es = ExitStack()
        self.eng = {"pe": nc.tensor, "act": nc.scalar, "dve": nc.vector,
                    "pool": nc.gpsimd, "sp": nc.sync}
        self.sem = {k: self.es.enter_context(nc.semaphore("s_" + k)) for k in ENGS}
        self.ops = {k: [] for k in ENGS}
        self.seen = {k: {} for k in ENGS}
        self.nsem = 0
        self.out_events = []
        self.all_dma = {}
        self._psum_ids = set()

    def sbuf(self, name, shape, dtype):
        t = self.es.enter_context(self.nc.sbuf_tensor(name, list(shape), dtype))
        return t

    def psum(self, name, shape, dtype=F32):
        t = self.es.enter_context(self.nc.psum_tensor(name, list(shape), dtype))
        self._psum_ids.add(id(t))
        return t

    def reg(self, ap, name=""):
        r = R(ap, name)
        r.is_psum = id(ap) in self._psum_ids
        return r

    def alias(self, new_regions, old_regions):
        evs = []
        for o in old_regions:
            if o.last_w is not None:
                evs.append(o.last_w)
            evs.extend(o.reads)
        for n in new_regions:
            n.reads = list(evs) + n.reads
            n.is_psum = n.is_psum or any(o.is_psum for o in old_regions)

    def _newsem(self, name):
        self.nsem += 1
        return self.es.enter_context(self.nc.semaphore("d_%d" % self.nsem))

    def _collect(self, eng, reads, writes):
        waits = {}

        def need(ev, raw):
            if ev is None:
                return
            key, val = ev
            if key == eng and eng == "pe" and not raw:
                return
            if waits.get(key, 0) < val:
                waits[key] = val

        for t in reads:
            need(t.last_w, True)
            if t.is_psum:
                for ev in t.reads:
                    if ev[0] != eng:
                        need(ev, True)
        for t in writes:
            need(t.last_w, False)
            for ev in t.reads:
                need(ev, False)
        wl = []
        for key, val in waits.items():
            if self.seen[eng].get(key, 0) >= val:
                continue
            self.seen[eng][key] = val
            wl.append((key, val))
            if isinstance(key, str):
                self.ops[key][val - 1][2] = True
        return wl

    def op(self, eng, fn, reads=(), writes=(), **kw):
        if isinstance(fn, str):
            name, kws = fn, kw
            fn = lambda e, name=name, kws=kws: getattr(e, name)(**kws)
        wl = self._collect(eng, reads, writes)
        self.ops[eng].append([wl, fn, False, "c"])
        ev = (eng, len(self.ops[eng]))
        for t in reads:
            t.reads.append(ev)
        for t in writes:
            t.last_w = ev
            t.reads = []
        return ev

    def dma(self, q, out, in_, reads=(), writes=(), semreg=None, is_out=False, **kw):
        wl = self._collect(q, reads, writes)
        if semreg is None:
            semreg = writes[0] if writes else reads[0]
        if semreg.dsem is None:
            semreg.dsem = self._newsem(semreg.name)
        semreg.dcnt += 16
        sem = semreg.dsem
        self.ops[q].append([wl, (out, in_, kw, sem), False, "d"])
        self.all_dma[sem] = semreg.dcnt
        ev = (sem, semreg.dcnt)
        for t in reads:
            t.reads.append(ev)
        for t in writes:
            t.last_w = ev
            t.reads = []
        if is_out:
            self.out_events.append(ev)
        return ev

    def wait_all_outputs(self, eng="sp"):
        waits = dict(self.all_dma)
        self.ops[eng].append([list(waits.items()), None, False, "w"])

    def finalize(self):
        nc = self.nc
        semval = {}
        for k in ENGS:
            c = 0
            vals = []
            for o in self.ops[k]:
                if o[3] == "c" and o[2]:
                    c += 1
                vals.append(c)
            semval[k] = vals
        ctx = self

        def run(k, e):
            for i, (wl, fn, flagged, kind) in enumerate(ctx.ops[k]):
                for key, val in wl:
                    if isinstance(key, str):
                        e.wait_ge(ctx.sem[key], semval[key][val - 1])
                    else:
                        e.wait_ge(key, val)
                if kind == "c":
                    ins = fn(e)
                    if flagged:
                        ins.then_inc(ctx.sem[k], 1)
                elif kind == "d":
                    out, in_, kw, sem = fn
                    e.dma_start(out=out, in_=in_, **kw).then_inc(sem, 16)

        with nc.Block() as block:
            @block.sync
            def _(e):
                run("sp", e)

            @block.scalar
            def _(e):
                run("act", e)

            @block.vector
            def _(e):
                run("dve", e)

            @block.gpsimd
            def _(e):
                run("pool", e)

            @block.tensor
            def _(e):
                run("pe", e)
        self.es.close()


D = 2048
NC16 = 16
NT = 1024
TT = 512
NTT = NT // TT
DFF = 8192
RMS_EPS = 1e-6


class WStream:
    def __init__(self, c, nslot=3, q="pool"):
        self.c = c
        self.q = q
        self.slots = []
        for i in range(nslot):
            t = c.sbuf("wslot%d" % i, [128, 16, 512], BF16)
            self.slots.append(c.reg(t, "wslot%d" % i))
        self.i = 0

    def load(self, w_ap, row0, col0, nrows=2048, ncols=512):
        s = self.slots[self.i % len(self.slots)]
        self.i += 1
        src = w_ap[row0:row0 + nrows, col0:col0 + ncols].rearrange("(c p) f -> p c f", p=128)
        nck = nrows // 128
        self.c.dma(self.q, s[:, 0:nck, 0:ncols], src, writes=[s])
        return s


class PsumPool:
    def __init__(self, c, n=8, prefix="ps"):
        self.banks = [c.reg(c.psum("%s%d" % (prefix, i), [128, 512]), "%s%d" % (prefix, i)) for i in range(n)]
        self.i = 0

    def next(self):
        b = self.banks[self.i % len(self.banks)]
        self.i += 1
        return b


def emit_rmsnorm(c, pp, resid, sq, ones, gcol, rstd, out, tmp, in_place=False):
    for tt in range(NTT):
        ts = slice(tt * TT, (tt + 1) * TT)
        for ch in range(NC16):
            c.op("act", "activation", out=sq[ch][tt][:], in_=resid[ch][:, ts], func=AF.Square,
                 reads=[resid[ch]], writes=[sq[ch][tt]])
        ps = pp.next()
        for ch in range(NC16):
            c.op("pe", "matmul", out=ps[:], lhsT=ones[:], rhs=sq[ch][tt][:], start=(ch == 0), stop=(ch == NC16 - 1),
                 reads=[ones, sq[ch][tt]], writes=[ps])
        c.op("dve", "tensor_scalar", out=tmp[:], in0=ps[:], scalar1=1.0 / D, scalar2=RMS_EPS, op0=ALU.mult, op1=ALU.add,
             reads=[ps], writes=[tmp])
        c.op("act", "activation", out=tmp[:], in_=tmp[:], func=AF.Sqrt, reads=[tmp], writes=[tmp])
        c.op("dve", "reciprocal", out=rstd[tt][:], in_=tmp[:], reads=[tmp], writes=[rstd[tt]])
        for ch in range(NC16):
            if in_place:
                c.op("dve", "scalar_tensor_tensor", out=resid[ch][:, ts], in0=resid[ch][:, ts], scalar=gcol[:, ch:ch + 1], in1=rstd[tt][:], op0=ALU.mult, op1=ALU.mult,
                     reads=[resid[ch], gcol, rstd[tt]], writes=[resid[ch]])
            else:
                c.op("dve", "scalar_tensor_tensor", out=out[ch][tt][:], in0=resid[ch][:, ts], scalar=gcol[:, ch:ch + 1], in1=rstd[tt][:], op0=ALU.mult, op1=ALU.mult,
                     reads=[resid[ch], gcol, rstd[tt]], writes=[out[ch][tt]])


def emit_proj_add(c, pp, ws, w_ap, row0, src, resid, nk=16):
    for og in range(4):
        slot = ws.load(w_ap, row0, og * 512, nrows=nk * 128)
        for ocl in range(4):
            oc = og * 4 + ocl
            for tt in range(NTT):
                ts = slice(tt * TT, (tt + 1) * TT)
                ps = pp.next()
                for k in range(nk):
                    c.op("pe", "matmul", out=ps[:], lhsT=slot[:, k, ocl * 128:(ocl + 1) * 128], rhs=src[k][tt][:], start=(k == 0), stop=(k == nk - 1),
                         reads=[slot, src[k][tt]], writes=[ps])
                c.op("dve", "tensor_tensor", out=resid[oc][:, ts], in0=ps[:], in1=resid[oc][:, ts], op=ALU.add,
                     reads=[ps, resid[oc]], writes=[resid[oc]])


def emit_ffn(c, pp, ws, w1_ap, w2_ap, xn, hid, resid, rtmp):
    for q in range(4):
        for fg in range(4):
            slot = ws.load(w1_ap, 0, q * 2048 + fg * 512)
            for fcl in range(4):
                fc = fg * 4 + fcl
                for tt in range(NTT):
                    ps = pp.next()
                    for k in range(NC16):
                        c.op("pe", "matmul", out=ps[:], lhsT=slot[:, k, fcl * 128:(fcl + 1) * 128], rhs=xn[k][tt][:], start=(k == 0), stop=(k == NC16 - 1),
                             reads=[slot, xn[k][tt]], writes=[ps])
                    rt = rtmp[(fc * NTT + tt) % len(rtmp)]
                    c.op("act", "activation", out=rt[:], in_=ps[:], func=AF.Relu, reads=[ps], writes=[rt])
                    c.op("dve", "tensor_tensor", out=hid[fc][tt][:], in0=rt[:], in1=rt[:], op=ALU.mult,
                         reads=[rt], writes=[hid[fc][tt]])
        emit_proj_add(c, pp, ws, w2_ap, q * 2048, hid, resid)


def build_ff(final=False):
    nc = bass.Bass("TRN2", target_bir_lowering=False)
    hT = nc.dram_tensor("hT", [D, NT], F32, kind="ExternalInput").ap()
    mT = nc.dram_tensor("mT", [D, NT], BF16, kind="ExternalInput").ap()
    wo = nc.dram_tensor("wo", [D, D], F32, kind="ExternalInput").ap()
    w1 = nc.dram_tensor("w1", [D, DFF], F32, kind="ExternalInput").ap()
    w2 = nc.dram_tensor("w2", [DFF, D], F32, kind="ExternalInput").ap()
    g = nc.dram_tensor("g", [128, 32], F32, kind="ExternalInput").ap()
    ones_in = nc.dram_tensor("ones", [128, 128], BF16, kind="ExternalInput").ap()
    oT = nc.dram_tensor("oT", [D, NT], F32, kind="ExternalOutput").ap()

    c = Ctx(nc)
    resid_t = c.sbuf("resid", [128, NC16, NT], F32)
    resid = [c.reg(resid_t[:, ch, :], "resid%d" % ch) for ch in range(NC16)]
    xn_t = c.sbuf("xn", [128, NC16, NT], BF16)
    xn = [[c.reg(xn_t[:, ch, tt * TT:(tt + 1) * TT], "xn") for tt in range(NTT)] for ch in range(NC16)]
    big_t = c.sbuf("big", [128, NC16, NT], BF16)
    big = [[c.reg(big_t[:, ch, tt * TT:(tt + 1) * TT], "big") for tt in range(NTT)] for ch in range(NC16)]
    gcol = c.reg(c.sbuf("gcol", [128, 32], F32), "gcol")
    ones = c.reg(c.sbuf("ones_sb", [128, 128], BF16), "ones")
    rstd = [c.reg(c.sbuf("rstd%d" % tt, [128, TT], F32), "rstd") for tt in range(NTT)]
    tmp = c.reg(c.sbuf("ntmp", [128, TT], F32), "ntmp")
    rtmp = [c.reg(c.sbuf("rtmp%d" % i, [128, TT], F32), "rtmp") for i in range(2)]
    ws = WStream(c, nslot=3)
    pp = PsumPool(c, 8)

    c.dma("sp", gcol[:], g, writes=[gcol])
    c.dma("sp", ones[:], ones_in, writes=[ones])
    hv = hT.rearrange("(c p) t -> p c t", p=128)
    for ch in range(NC16):
        c.dma("sp" if ch % 2 == 0 else "act", resid[ch][:], hv[:, ch, :], writes=[resid[ch]])
    mv = mT.rearrange("(c p) t -> p c t", p=128)
    for ch in range(NC16):
        c.dma("act" if ch % 2 == 0 else "sp", big_t[:, ch, :], mv[:, ch, :], writes=[big[ch][0], big[ch][1]])

    emit_proj_add(c, pp, ws, wo, 0, big, resid)
    gl = c.reg(gcol[:, 0:16], "gl")
    gl.last_w = gcol.last_w
    emit_rmsnorm(c, pp, resid, big, ones, gcol, rstd, xn, tmp)
    emit_ffn(c, pp, ws, w1, w2, xn, big, resid, rtmp)
    if final:
        gf = c.reg(gcol[:, 16:32], "gf")
        gf.last_w = gcol.last_w
        emit_rmsnorm(c, pp, resid, big, ones, gf, rstd, None, tmp, in_place=True)
    ov = oT.rearrange("(c p) t -> p c t", p=128)
    for ch in range(NC16):
        c.dma("sp" if ch % 2 == 0 else "act", ov[:, ch, :], resid[ch][:], reads=[resid[ch]], is_out=True)
    c.wait_all_outputs("sp")
    c.wait_all_outputs("act")
    c.finalize()
    return nc


FL = 1024
NFC = 8
STW = 256
CH = 128
NJ = STW // CH
GN_EPS = 64e-5
WSCALE = -0.6065306597126334

P_W0, P_A0, P_V0, P_KK, P_KA, P_RK, P_GW, P_GB, P_OMKA = range(9)
MU_ORDER = {"r": 0, "w": 1, "k": 2, "v": 3, "a": 4, "g": 5}


def build_rw(layer, T=2048, stop=99):
    NST = T // STW
    nc = bass.Bass("TRN2", target_bir_lowering=False)

    def din(name, shape, dt=F32):
        return nc.dram_tensor(name, list(shape), dt, kind="ExternalInput").ap()

    hT = din("hT", [D, T])
    gmix_d = din("gmix", [128, 16])
    mu_d = din("mu", [128, 96])
    wr_d, wk_d, wv_d = din("wr", [D, FL]), din("wk", [D, FL]), din("wv", [D, FL])
    w1_d, a1_d, g1_d = din("w1", [D, 96]), din("a1", [D, 96]), din("g1", [D, 256])
    w2_d, a2_d, g2_d = din("w2", [96, FL]), din("a2", [96, FL]), din("g2", [256, FL])
    pv_d = din("pv", [128, 72])
    if layer == 1:
        v1_d, v2_d = din("v1", [D, 64]), din("v2", [64, FL])
        vfT = din("vfT", [FL, T])
    ident_d = din("ident", [128, 128], BF16)
    onesb_d = din("onesb", [128, 128], BF16)
    bones_d = din("bones", [128, 128], F32)
    onesf_d = din("onesf", [128, 128], F32)
    mask3_d = din("mask3", [128, 384], BF16)
    masksu_d = din("masksu", [128, 128], BF16)
    masksl_d = din("masksl", [128, 128], BF16)
    ygT = nc.dram_tensor("ygT", [FL, T], BF16, kind="ExternalOutput").ap()
    if layer == 0:
        vT_o = nc.dram_tensor("vT", [FL, T], F32, kind="ExternalOutput").ap()

    c = Ctx(nc)

    def sb(name, shape, dt):
        return c.sbuf("sb_" + name, shape, dt)

    def rg(ap, name=""):
        return c.reg(ap, name)

    ident = rg(sb("ident", [128, 128], BF16), "ident")
    onesb = rg(sb("onesb", [128, 128], BF16), "onesb")
    bones = rg(sb("bones", [128, 128], F32), "bones")
    onesf = rg(sb("onesf", [128, 128], F32), "onesf")
    mask3 = rg(sb("mask3", [128, 384], BF16), "mask3")
    masksu = rg(sb("masksu", [128, 128], BF16), "masksu")
    masksl = rg(sb("masksl", [128, 128], BF16), "masksl")
    gmix = rg(sb("gmix", [128, 16], F32), "gmix")
    mu = rg(sb("mu", [128, 96], F32), "mu")
    omm = rg(sb("omm", [128, 96], F32), "omm")
    pv = rg(sb("pv", [128, 72], F32), "pv")
    for r_, d_ in ((ident, ident_d), (onesb, onesb_d), (bones, bones_d), (onesf, onesf_d), (mask3, mask3_d),
                   (masksu, masksu_d), (masksl, masksl_d), (gmix, gmix_d), (mu, mu_d), (pv, pv_d)):
        c.dma("sp", r_[:], d_, writes=[r_])
    c.op("dve", "tensor_scalar", out=omm[:], in0=mu[:], scalar1=-1.0, scalar2=1.0, op0=ALU.mult, op1=ALU.add, reads=[mu], writes=[omm])
    c.op("dve", "tensor_scalar", out=pv[:, P_OMKA * 8:(P_OMKA + 1) * 8], in0=pv[:, P_KA * 8:(P_KA + 1) * 8], scalar1=-1.0, scalar2=1.0,
         op0=ALU.mult, op1=ALU.add, reads=[pv], writes=[pv])

    def pcol(p, fc):
        return pv[:, p * 8 + fc:p * 8 + fc + 1]

    w2sb = rg(sb("w2sb", [96, FL], BF16), "w2sb")
    a2sb = rg(sb("a2sb", [96, FL], BF16), "a2sb")
    g2sb = rg(sb("g2sb", [128, 2, FL], BF16), "g2sb")
    c.dma("pool", w2sb[:], w2_d, writes=[w2sb])
    c.dma("pool", a2sb[:], a2_d, writes=[a2sb])
    c.dma("pool", g2sb[:], g2_d.rearrange("(m p) f -> p m f", p=128), writes=[g2sb])
    if layer == 1:
        v2sb = rg(sb("v2sb", [64, FL], BF16), "v2sb")
        c.dma("pool", v2sb[:], v2_d, writes=[v2sb])

    hn_t = sb("hn", [128, 16, STW + 1], F32)
    hn = [rg(hn_t[:, ch, :], "hn%d" % ch) for ch in range(16)]
    carry_t = sb("carry", [128, 16, 1], F32)
    carry = rg(carry_t, "carry")
    xbuf_t = [sb("xbuf0", [128, 16, STW], BF16)]
    xbuf = [[rg(xbuf_t[0][:, ch, :], "x0_%d" % ch) for ch in range(16)]]
    xx_t = sb("xx", [128, 16, STW], BF16)
    xx = [rg(xx_t[:, ch, :], "xx%d" % ch) for ch in range(16)]
    rstd = rg(sb("rstd", [128, STW], F32), "rstd")

    def fcbuf(name, dt):
        t = sb(name, [128, NFC, STW], dt)
        return t, [rg(t[:, fc, :], "%s%d" % (name, fc)) for fc in range(NFC)]

    a_tt, a_t = fcbuf("a_t", BF16)
    r_tt, r_t = fcbuf("r_t", BF16)
    Epos_t, Epos = fcbuf("Epos", BF16)
    Eneg_t, Eneg = fcbuf("Eneg", BF16)
    Eprev_t, Eprev = fcbuf("Eprev", BF16)
    rT_t, rT = fcbuf("rT", BF16)
    kT_t, kT = fcbuf("kT", BF16)
    aT_t, aT = fcbuf("aT", BF16)
    bT_t, bT = fcbuf("bT", BF16)
    vb_t, vb = fcbuf("vb", BF16)
    bonus_t, bonus = fcbuf("bonus", BF16)
    bsum_t, bsum = fcbuf("bsum", BF16)
    gst_t, gst = fcbuf("gst", BF16)
    gC_t = sb("gC", [128, NFC, NJ], F32)
    gC = [rg(gC_t[:, fc, :], "gC%d" % fc) for fc in range(NFC)]
    khat_t = sb("khat", [128, NJ, FL], BF16)
    bhat_t = sb("bhat", [128, NJ, FL], BF16)
    Vtm_t = sb("Vtm", [128, NJ, FL], BF16)
    khat = [rg(khat_t[:, j, :], "khat%d" % j) for j in range(NJ)]
    bhat = [rg(bhat_t[:, j, :], "bhat%d" % j) for j in range(NJ)]
    Vtm = [rg(Vtm_t[:, j, :], "Vtm%d" % j) for j in range(NJ)]
    yout_t = sb("yout", [128, NFC, STW], BF16)
    yout = [rg(yout_t[:, :, j * CH:(j + 1) * CH], "yout%d" % j) for j in range(NJ)]

    NTMP = 7
    tmps = [rg(sb("tmp%d" % i, [128, STW], F32), "tmp%d" % i) for i in range(NTMP)]
    tctr = [0]

    def tmp():
        t = tmps[tctr[0] % NTMP]
        tctr[0] += 1
        return t

    t1b = rg(sb("t1b", [128, 2, STW], BF16), "t1b")

    Aseq_t = sb("Aseq", [128, 16, 512], BF16)
    Aseq = [rg(Aseq_t[:, h, :], "Aseq%d" % h) for h in range(16)]
    Pg = [rg(sb("Pg%d" % i, [128, 512], F32), "Pg%d" % i) for i in range(2)]
    PTg = [rg(sb("PTg%d" % i, [128, 512], F32), "PTg%d" % i) for i in range(2)]
    XTg = [rg(sb("XTg%d" % i, [128, 512], F32), "XTg%d" % i) for i in range(2)]
    Zb = rg(sb("Zb", [128, FL], BF16), "Zb")
    Ub = rg(sb("Ub", [128, FL], BF16), "Ub")
    ST32_t = sb("ST32", [128, NFC, 64], F32)
    ST32 = rg(ST32_t, "ST32")
    STpad_t = sb("STpad", [128, 16, 64], BF16)
    STe = rg(STpad_t[:, 0:16:2, :], "STe")
    STo = rg(STpad_t[:, 1:16:2, :], "STo")
    Ysb_t = sb("Ysb", [128, NFC, CH], F32)
    Ysb = rg(Ysb_t, "Ysb")
    Ysq_t = sb("Ysq", [128, NFC, CH], F32)
    Ysq = rg(Ysq_t, "Ysq")
    mean_t = sb("mean", [128, NFC, CH], F32)
    mean = rg(mean_t, "mean")
    var_t = sb("var", [128, NFC, CH], F32)
    var = rg(var_t, "var")

    ws = WStream(c, nslot=2)
    pp = PsumPool(c, 7)
    ptb_t = c.psum("ptb", [128, 1024], BF16)
    ptb = rg(ptb_t, "ptb")

    c.op("pool", "memset", ap=ST32[:], constant=0.0, writes=[ST32])
    c.op("pool", "memset", ap=STpad_t[:], constant=0.0, writes=[STe, STo])

    hv = hT.rearrange("(c p) t -> p c t", p=128)

    def proj_fc(w_d, cg, xs):
        slot = ws.load(w_d, 0, cg * 512)
        for fcl in range(4):
            fc = cg * 4 + fcl
            ps = pp.next()
            for k in range(16):
                c.op("pe", "matmul", out=ps[:, 0:STW], lhsT=slot[:, k, fcl * 128:(fcl + 1) * 128], rhs=xs[k][:],
                     start=(k == 0), stop=(k == 15), reads=[slot, xs[k]], writes=[ps])
            yield fc, ps

    for st in range(NST):
        t0 = st * STW
        c.dma("sp", hn_t[:, :, 1:STW + 1], hv[:, :, t0:t0 + STW], writes=hn)
        if st == 0:
            c.op("pool", "memset", ap=hn_t[:, :, 0:1], constant=0.0, writes=hn)
        else:
            c.op("pool", "tensor_copy", out=hn_t[:, :, 0:1], in_=carry[:], reads=[carry], writes=hn)
        sq = xbuf[0]
        for ch in range(16):
            c.op("act", "activation", out=sq[ch][:], in_=hn[ch][:, 1:STW + 1], func=AF.Square, reads=[hn[ch]], writes=[sq[ch]])
        ps = pp.next()
        for ch in range(16):
            c.op("pe", "matmul", out=ps[:, 0:STW], lhsT=onesb[:], rhs=sq[ch][:], start=(ch == 0), stop=(ch == 15),
                 reads=[onesb, sq[ch]], writes=[ps])
        tt = tmp()
        c.op("dve", "tensor_scalar", out=tt[:], in0=ps[:, 0:STW], scalar1=1.0 / D, scalar2=RMS_EPS, op0=ALU.mult, op1=ALU.add,
             reads=[ps], writes=[tt])
        c.op("act", "activation", out=tt[:], in_=tt[:], func=AF.Sqrt, reads=[tt], writes=[tt])
        c.op("dve", "reciprocal", out=rstd[:], in_=tt[:], reads=[tt], writes=[rstd])
        for ch in range(16):
            c.op("dve", "scalar_tensor_tensor", out=hn[ch][:, 1:STW + 1], in0=hn[ch][:, 1:STW + 1], scalar=gmix[:, ch:ch + 1],
                 in1=rstd[:], op0=ALU.mult, op1=ALU.mult, reads=[hn[ch], gmix, rstd], writes=[hn[ch]])
        c.op("pool", "tensor_copy", out=carry[:], in_=hn_t[:, :, STW:STW + 1], reads=hn, writes=[carry])
        for ch in range(16):
            c.op("dve", "tensor_tensor", out=xx[ch][:], in0=hn[ch][:, 0:STW], in1=hn[ch][:, 1:STW + 1], op=ALU.subtract, reads=[hn[ch]], writes=[xx[ch]])

        xi = [0]

        def lerp(name):
            i = MU_ORDER[name]
            xs = xbuf[0]
            xi[0] += 1
            for ch in range(16):
                c.op("dve", "scalar_tensor_tensor", out=xs[ch][:], in0=xx[ch][:], scalar=mu[:, i * 16 + ch:i * 16 + ch + 1],
                     in1=hn[ch][:, 1:STW + 1], op0=ALU.mult, op1=ALU.add, reads=[xx[ch], mu, hn[ch]], writes=[xs[ch]])
            return xs

        def lora1(w_d, ncols, xs, func, m=0):
            slot = ws.load(w_d, 0, m * 128, ncols=ncols)
            ps = pp.next()
            for k in range(16):
                c.op("pe", "matmul", out=ps[0:ncols, 0:STW], lhsT=slot[:, k, 0:ncols], rhs=xs[k][:], start=(k == 0), stop=(k == 15),
                     reads=[slot, xs[k]], writes=[ps])
            c.op("act", "activation", out=t1b[0:ncols, m, :], in_=ps[0:ncols, 0:STW], func=func, reads=[ps], writes=[t1b])

        if stop <= 0:
            break
        xs = lerp("w")
        lora1(w1_d, 96, xs, AF.Tanh)
        for fc in range(NFC):
            ps = pp.next()
            c.op("pe", "matmul", out=ps[:, 0:STW], lhsT=w2sb[0:96, fc * 128:(fc + 1) * 128], rhs=t1b[0:96, 0, :], start=True, stop=True,
                 reads=[w2sb, t1b], writes=[ps])
            lw = tmp()
            c.op("act", "activation", out=lw[:], in_=ps[:, 0:STW], func=AF.Sigmoid, bias=pcol(P_W0, fc), reads=[ps, pv], writes=[lw])
            c.op("dve", "tensor_single_scalar", out=lw[:], in_=lw[:], scalar=WSCALE, op=ALU.mult, reads=[lw], writes=[lw])
            cum = tmp()
            for j in range(NJ):
                c.op("dve", "tensor_tensor_scan", out=cum[:, j * CH:(j + 1) * CH], data0=onesf[:, 0:CH], data1=lw[:, j * CH:(j + 1) * CH],
                     initial=0.0, op0=ALU.mult, op1=ALU.add, reads=[onesf, lw], writes=[cum])
            ep = tmp()
            c.op("act", "activation", out=ep[:], in_=cum[:], func=AF.Exp, reads=[cum], writes=[ep])
            c.op("pool", "tensor_copy", out=Epos[fc][:], in_=ep[:], reads=[ep], writes=[Epos[fc]])
            c.op("dve", "tensor_copy", out=gC[fc][:], in_=ep[:, CH - 1:STW:CH], reads=[ep], writes=[gC[fc]])
            c.op("act", "activation", out=Eneg[fc][:], in_=cum[:], func=AF.Exp, scale=-1.0, reads=[cum], writes=[Eneg[fc]])
            cml = tmp()
            c.op("dve", "tensor_tensor", out=cml[:], in0=cum[:], in1=lw[:], op=ALU.subtract, reads=[cum, lw], writes=[cml])
            c.op("act", "activation", out=Eprev[fc][:], in_=cml[:], func=AF.Exp, reads=[cml], writes=[Eprev[fc]])

        if stop <= 1:
            break
        xs = lerp("a")
        lora1(a1_d, 96, xs, AF.Copy)
        for fc in range(NFC):
            ps = pp.next()
            c.op("pe", "matmul", out=ps[:, 0:STW], lhsT=a2sb[0:96, fc * 128:(fc + 1) * 128], rhs=t1b[0:96, 0, :], start=True, stop=True,
                 reads=[a2sb, t1b], writes=[ps])
            c.op("act", "activation", out=a_t[fc][:], in_=ps[:, 0:STW], func=AF.Sigmoid, bias=pcol(P_A0, fc), reads=[ps, pv], writes=[a_t[fc]])

        if stop <= 2:
            break
        xs = lerp("r")
        for cg in range(2):
            for fc, ps in proj_fc(wr_d, cg, xs):
                c.op("act", "activation", out=r_t[fc][:], in_=ps[:, 0:STW], func=AF.Copy, reads=[ps], writes=[r_t[fc]])
                c.op("dve", "tensor_tensor", out=rT[fc][:], in0=ps[:, 0:STW], in1=Epos[fc][:], op=ALU.mult, reads=[ps, Epos[fc]], writes=[rT[fc]])

        if stop <= 3:
            break
        xs = lerp("k")
        for cg in range(2):
            for fc, ps in proj_fc(wk_d, cg, xs):
                kkr = tmp()
                c.op("dve", "tensor_single_scalar", out=kkr[:], in_=ps[:, 0:STW], scalar=pcol(P_KK, fc), op=ALU.mult,
                     reads=[ps, pv], writes=[kkr])
                ksq = tmp()
                c.op("act", "activation", out=ksq[:], in_=kkr[:], func=AF.Square, reads=[kkr], writes=[ksq])
                ps2 = pp.next()
                c.op("pe", "matmul", out=ps2[:, 0:STW], lhsT=bones[:], rhs=ksq[:], start=True, stop=True, reads=[bones, ksq], writes=[ps2])
                rn = tmp()
                c.op("act", "activation", out=rn[:], in_=ps2[:, 0:STW], func=AF.Sqrt, reads=[ps2], writes=[rn])
                c.op("dve", "tensor_single_scalar", out=rn[:], in_=rn[:], scalar=1e-12, op=ALU.max, reads=[rn], writes=[rn])
                c.op("dve", "reciprocal", out=rn[:], in_=rn[:], reads=[rn], writes=[rn])
                kk = tmp()
                c.op("dve", "tensor_tensor", out=kk[:], in0=kkr[:], in1=rn[:], op=ALU.mult, reads=[kkr, rn], writes=[kk])
                fac = tmp()
                c.op("dve", "tensor_scalar", out=fac[:], in0=a_t[fc][:], scalar1=pcol(P_KA, fc), scalar2=pcol(P_OMKA, fc), op0=ALU.mult, op1=ALU.add,
                     reads=[a_t[fc], pv], writes=[fac])
                kmod = tmp()
                c.op("dve", "tensor_tensor", out=kmod[:], in0=ps[:, 0:STW], in1=fac[:], op=ALU.mult, reads=[ps, fac], writes=[kmod])
                rk = fac
                c.op("dve", "scalar_tensor_tensor", out=rk[:], in0=kmod[:], scalar=pcol(P_RK, fc), in1=r_t[fc][:], op0=ALU.mult, op1=ALU.mult,
                     reads=[kmod, pv, r_t[fc]], writes=[rk])
                ps3 = pp.next()
                c.op("pe", "matmul", out=ps3[:, 0:STW], lhsT=bones[:], rhs=rk[:], start=True, stop=True, reads=[bones, rk], writes=[ps3])
                c.op("act", "activation", out=bsum[fc][:], in_=ps3[:, 0:STW], func=AF.Copy, reads=[ps3], writes=[bsum[fc]])
                c.op("dve", "tensor_tensor", out=kT[fc][:], in0=kmod[:], in1=Eneg[fc][:], op=ALU.mult, reads=[kmod, Eneg[fc]], writes=[kT[fc]])
                c.op("dve", "scalar_tensor_tensor", out=aT[fc][:], in0=kk[:], scalar=-1.0, in1=Eprev[fc][:], op0=ALU.mult, op1=ALU.mult,
                     reads=[kk, Eprev[fc]], writes=[aT[fc]])
                bq = kkr
                c.op("dve", "tensor_tensor", out=bq[:], in0=kk[:], in1=a_t[fc][:], op=ALU.mult, reads=[kk, a_t[fc]], writes=[bq])
                c.op("dve", "tensor_tensor", out=bT[fc][:], in0=bq[:], in1=Eneg[fc][:], op=ALU.mult, reads=[bq, Eneg[fc]], writes=[bT[fc]])

        if stop <= 4:
            break
        xs = lerp("v")
        if layer == 1:
            lora1(v1_d, 64, xs, AF.Copy)
        for cg in range(2):
            for fc, ps in proj_fc(wv_d, cg, xs):
                vfin = tmp()
                if layer == 0:
                    c.op("act", "activation", out=vfin[:], in_=ps[:, 0:STW], func=AF.Copy, reads=[ps], writes=[vfin])
                    c.dma("act", vT_o[fc * 128:(fc + 1) * 128, t0:t0 + STW], vfin[:], reads=[vfin], is_out=True)
                else:
                    psv = pp.next()
                    c.op("pe", "matmul", out=psv[:, 0:STW], lhsT=v2sb[0:64, fc * 128:(fc + 1) * 128], rhs=t1b[0:64, 0, :], start=True, stop=True,
                         reads=[v2sb, t1b], writes=[psv])
                    sg = tmp()
                    c.op("act", "activation", out=sg[:], in_=psv[:, 0:STW], func=AF.Sigmoid, bias=pcol(P_V0, fc), reads=[psv, pv], writes=[sg])
                    vfc = tmp()
                    c.dma("sp", vfc[:], vfT[fc * 128:(fc + 1) * 128, t0:t0 + STW], writes=[vfc])
                    dd = tmp()
                    c.op("dve", "tensor_tensor", out=dd[:], in0=vfc[:], in1=ps[:, 0:STW], op=ALU.subtract, reads=[vfc, ps], writes=[dd])
                    c.op("dve", "tensor_tensor", out=dd[:], in0=dd[:], in1=sg[:], op=ALU.mult, reads=[dd, sg], writes=[dd])
                    c.op("dve", "tensor_tensor", out=vfin[:], in0=ps[:, 0:STW], in1=dd[:], op=ALU.add, reads=[ps, dd], writes=[vfin])
                c.op("dve", "tensor_tensor", out=bonus[fc][:], in0=vfin[:], in1=bsum[fc][:], op=ALU.mult, reads=[vfin, bsum[fc]], writes=[bonus[fc]])
                c.op("pool", "tensor_copy", out=vb[fc][:], in_=vfin[:], reads=[vfin], writes=[vb[fc]])

        if stop <= 5:
            break
        xs = lerp("g")
        for m in range(2):
            lora1(g1_d, 128, xs, AF.Sigmoid, m=m)
        for fc in range(NFC):
            ps = pp.next()
            for m in range(2):
                c.op("pe", "matmul", out=ps[:, 0:STW], lhsT=g2sb[:, m, fc * 128:(fc + 1) * 128], rhs=t1b[:, m, :], start=(m == 0), stop=(m == 1),
                     reads=[g2sb, t1b], writes=[ps])
            c.op("act", "activation", out=gst[fc][:], in_=ps[:, 0:STW], func=AF.Copy, reads=[ps], writes=[gst[fc]])

        if stop <= 6:
            break
        for src, dst in ((kT, khat), (bT, bhat), (vb, Vtm)):
            for j in range(NJ):
                for fc in range(NFC):
                    c.op("pe", "transpose", out=ptb[:, fc * 128:(fc + 1) * 128], in_=src[fc][:, j * CH:(j + 1) * CH], identity=ident[:],
                         reads=[src[fc], ident], writes=[ptb])
                c.op("act", "activation", out=dst[j][:], in_=ptb[:], func=AF.Copy, reads=[ptb], writes=[dst[j]])

        if stop <= 7:
            break
        for j in range(NJ):
            chs = slice(j * CH, (j + 1) * CH)
            for G in range(4):
                cur = 0
                for hl in range(4):
                    h = G * 4 + hl
                    fc = h // 2
                    hs = slice((h % 2) * 64, (h % 2) * 64 + 64)
                    pa = pp.next()
                    rd = [kT[fc], aT[fc], rT[fc], bT[fc]]
                    c.op("pe", "matmul", out=pa[:, 0:128], lhsT=kT[fc][hs, chs], rhs=aT[fc][hs, chs], start=True, stop=True, reads=rd, writes=[pa])
                    c.op("pe", "matmul", out=pa[:, 128:256], lhsT=kT[fc][hs, chs], rhs=rT[fc][hs, chs], start=True, stop=True, reads=rd, writes=[pa])
                    c.op("pe", "matmul", out=pa[:, 256:384], lhsT=bT[fc][hs, chs], rhs=rT[fc][hs, chs], start=True, stop=True, reads=rd, writes=[pa])
                    c.op("dve", "tensor_tensor", out=Aseq[h][:, 0:384], in0=pa[:, 0:384], in1=mask3[:], op=ALU.mult, reads=[pa, mask3], writes=[Aseq[h]])
                    cs = slice(hl * 128, (hl + 1) * 128)
                    psNT = pp.next()
                    c.op("pe", "matmul", out=psNT[:, 0:128], lhsT=bT[fc][hs, chs], rhs=aT[fc][hs, chs], start=True, stop=True, reads=rd, writes=[psNT])
                    c.op("dve", "tensor_tensor", out=PTg[cur][:, cs], in0=psNT[:, 0:128], in1=masksu[:, 0:128], op=ALU.mult, reads=[psNT, masksu], writes=[PTg[cur]])
                    psN = pp.next()
                    c.op("pe", "matmul", out=psN[:, 0:128], lhsT=aT[fc][hs, chs], rhs=bT[fc][hs, chs], start=True, stop=True, reads=rd, writes=[psN])
                    c.op("dve", "tensor_tensor", out=Pg[cur][:, cs], in0=psN[:, 0:128], in1=masksl[:, 0:128], op=ALU.mult, reads=[psN, masksl], writes=[Pg[cur]])
                c.op("dve", "tensor_tensor", out=XTg[cur][:].rearrange("p (h t) -> p h t", h=4), in0=PTg[cur][:].rearrange("p (h t) -> p h t", h=4),
                     in1=ident[:].unsqueeze(1).to_broadcast([128, 4, 128]), op=ALU.add, reads=[PTg[cur], ident], writes=[XTg[cur]])
                for lvl in range(6):
                    nxt = 1 - cur
                    pP = pp.next()
                    pPT = pp.next()
                    for hl in range(4):
                        cs = slice(hl * 128, (hl + 1) * 128)
                        c.op("pe", "matmul", out=pP[:, cs], lhsT=PTg[cur][:, cs], rhs=Pg[cur][:, cs], start=True, stop=True, reads=[PTg[cur], Pg[cur]], writes=[pP])
                        c.op("pe", "matmul", out=pPT[:, cs], lhsT=Pg[cur][:, cs], rhs=PTg[cur][:, cs], start=True, stop=True, reads=[PTg[cur], Pg[cur]], writes=[pPT])
                    c.op("act", "activation", out=Pg[nxt][:], in_=pP[:], func=AF.Copy, reads=[pP], writes=[Pg[nxt]])
                    c.op("act", "activation", out=PTg[nxt][:], in_=pPT[:], func=AF.Copy, reads=[pPT], writes=[PTg[nxt]])
                    pX = pp.next()
                    for hl in range(4):
                        cs = slice(hl * 128, (hl + 1) * 128)
                        c.op("pe", "matmul", out=pX[:, cs], lhsT=Pg[nxt][:, cs], rhs=XTg[cur][:, cs], start=True, stop=True, reads=[Pg[nxt], XTg[cur]], writes=[pX])
                    if lvl < 5:
                        c.op("dve", "tensor_tensor", out=XTg[nxt][:], in0=pX[:], in1=XTg[cur][:], op=ALU.add, reads=[pX, XTg[cur]], writes=[XTg[nxt]])
                    else:
                        c.op("dve", "tensor_tensor", out=Aseq_t[:, G * 4:(G + 1) * 4, 384:512], in0=pX[:].rearrange("p (h t) -> p h t", h=4),
                             in1=XTg[cur][:].rearrange("p (h t) -> p h t", h=4), op=ALU.add, reads=[pX, XTg[cur]], writes=Aseq[G * 4:(G + 1) * 4])
                    cur = nxt

            if stop <= 8:
                break
            def stp(h):
                return STe if h % 2 == 0 else STo

            pz = [pp.next(), pp.next()]
            for h in range(16):
                fc = h // 2
                o = pz[h // 8][:, (h % 8) * 64:(h % 8) * 64 + 64]
                c.op("pe", "matmul", out=o, lhsT=aT[fc][:, chs], rhs=STpad_t[:, h, :], start=True, stop=False, reads=[aT[fc], stp(h)], writes=[pz[h // 8]])
                c.op("pe", "matmul", out=o, lhsT=Aseq[h][:, 0:128], rhs=Vtm[j][:, h * 64:(h + 1) * 64], start=False, stop=True, reads=[Aseq[h], Vtm[j]], writes=[pz[h // 8]])
            for b in range(2):
                c.op("act", "activation", out=Zb[:, b * 512:(b + 1) * 512], in_=pz[b][:], func=AF.Copy, reads=[pz[b]], writes=[Zb])
            pu = [pp.next(), pp.next()]
            for h in range(16):
                o = pu[h // 8][:, (h % 8) * 64:(h % 8) * 64 + 64]
                c.op("pe", "matmul", out=o, lhsT=Aseq[h][:, 384:512], rhs=Zb[:, h * 64:(h + 1) * 64], start=True, stop=True, reads=[Aseq[h], Zb], writes=[pu[h // 8]])
            for b in range(2):
                c.op("dve", "tensor_copy", out=Ub[:, b * 512:(b + 1) * 512], in_=pu[b][:], reads=[pu[b]], writes=[Ub])
            py = [pp.next(), pp.next()]
            for h in range(16):
                fc = h // 2
                hs = slice((h % 2) * 64, (h % 2) * 64 + 64)
                o = py[fc // 4][hs, (fc % 4) * 128:(fc % 4) * 128 + 128]
                c.op("pe", "matmul", out=o, lhsT=STpad_t[:, h, :], rhs=rT[fc][:, chs], start=True, stop=False, reads=[stp(h), rT[fc]], writes=[py[fc // 4]])
                c.op("pe", "matmul", out=o, lhsT=Vtm[j][:, h * 64:(h + 1) * 64], rhs=Aseq[h][:, 128:256], start=False, stop=False, reads=[Vtm[j], Aseq[h]], writes=[py[fc // 4]])
                c.op("pe", "matmul", out=o, lhsT=Ub[:, h * 64:(h + 1) * 64], rhs=Aseq[h][:, 256:384], start=False, stop=True, reads=[Ub, Aseq[h]], writes=[py[fc // 4]])
            pst = pp.next()
            for h in range(16):
                fc = h // 2
                hs = slice((h % 2) * 64, (h % 2) * 64 + 64)
                o = pst[hs, fc * 64:(fc + 1) * 64]
                c.op("pe", "matmul", out=o, lhsT=khat[j][:, h * 64:(h + 1) * 64], rhs=Vtm[j][:, h * 64:(h + 1) * 64], start=True, stop=False, reads=[khat[j], Vtm[j]], writes=[pst])
                c.op("pe", "matmul", out=o, lhsT=bhat[j][:, h * 64:(h + 1) * 64], rhs=Ub[:, h * 64:(h + 1) * 64], start=False, stop=True, reads=[bhat[j], Ub], writes=[pst])
            c.op("dve", "tensor_tensor", out=ST32[:], in0=pst[:].rearrange("p (f i) -> p f i", f=NFC), in1=ST32[:], op=ALU.add, reads=[pst, ST32], writes=[ST32])
            c.op("dve", "tensor_tensor", out=ST32[:], in0=ST32[:], in1=gC_t[:, :, j:j + 1].to_broadcast([128, NFC, 64]), op=ALU.mult, reads=[ST32] + gC, writes=[ST32])
            c.op("pool", "tensor_copy", out=STpad_t[0:64, 0:16:2, :], in_=ST32_t[0:64, :, :], reads=[ST32], writes=[STe])
            c.op("pool", "tensor_copy", out=STpad_t[64:128, 1:16:2, :], in_=ST32_t[64:128, :, :], reads=[ST32], writes=[STo])
            if stop <= 9:
                break
            for b in range(2):
                c.op("act", "activation", out=Ysb_t[:, b * 4:(b + 1) * 4, :], in_=py[b][:].rearrange("p (f t) -> p f t", f=4), func=AF.Copy, reads=[py[b]], writes=[Ysb])
            c.op("pool", "tensor_tensor", out=Ysq[:], in0=Ysb[:], in1=Ysb[:], op=ALU.mult, reads=[Ysb], writes=[Ysq])
            for b in range(2):
                pm = pp.next()
                pq = pp.next()
                c.op("pe", "matmul", out=pm[:], lhsT=bones[:], rhs=Ysb_t[:, b * 4:(b + 1) * 4, :], start=True, stop=True, reads=[bones, Ysb], writes=[pm])
                c.op("pe", "matmul", out=pq[:], lhsT=bones[:], rhs=Ysq_t[:, b * 4:(b + 1) * 4, :], start=True, stop=True, reads=[bones, Ysq], writes=[pq])
                mv = mean_t[:, b * 4:(b + 1) * 4, :]
                vv = var_t[:, b * 4:(b + 1) * 4, :]
                c.op("act", "activation", out=mv, in_=pm[:].rearrange("p (f t) -> p f t", f=4), func=AF.Copy, scale=1.0 / 64, reads=[pm], writes=[mean])
                c.op("dve", "tensor_tensor", out=vv, in0=mv, in1=mv, op=ALU.mult, reads=[mean], writes=[var])
                c.op("dve", "scalar_tensor_tensor", out=vv, in0=pq[:].rearrange("p (f t) -> p f t", f=4), scalar=1.0 / 64, in1=vv, op0=ALU.mult, op1=ALU.subtract,
                     reads=[pq, var], writes=[var])
            c.op("dve", "tensor_single_scalar", out=var[:], in_=var[:], scalar=GN_EPS, op=ALU.add, reads=[var], writes=[var])
            c.op("act", "activation", out=var[:], in_=var[:], func=AF.Sqrt, reads=[var], writes=[var])
            c.op("dve", "reciprocal", out=var[:], in_=var[:], reads=[var], writes=[var])
            c.op("dve", "tensor_tensor", out=Ysb[:], in0=Ysb[:], in1=mean[:], op=ALU.subtract, reads=[Ysb, mean], writes=[Ysb])
            c.op("dve", "tensor_tensor", out=Ysb[:], in0=Ysb[:], in1=var[:], op=ALU.mult, reads=[Ysb, var], writes=[Ysb])
            gw = pv[:, P_GW * 8:(P_GW + 1) * 8].unsqueeze(2).to_broadcast([128, NFC, CH])
            gb = pv[:, P_GB * 8:(P_GB + 1) * 8].unsqueeze(2).to_broadcast([128, NFC, CH])
            c.op("dve", "tensor_tensor", out=Ysb[:], in0=Ysb[:], in1=gw, op=ALU.mult, reads=[Ysb, pv], writes=[Ysb])
            c.op("dve", "tensor_tensor", out=Ysb[:], in0=Ysb[:], in1=gb, op=ALU.add, reads=[Ysb, pv], writes=[Ysb])
            c.op("dve", "tensor_tensor", out=Ysb[:], in0=Ysb[:], in1=bonus_t[:, :, chs], op=ALU.add, reads=[Ysb] + bonus, writes=[Ysb])
            c.op("dve", "tensor_tensor", out=yout[j][:], in0=Ysb[:], in1=gst_t[:, :, chs], op=ALU.mult, reads=[Ysb] + gst, writes=[yout[j]])
        if stop <= 9:
            break
        c.dma("sp", ygT.rearrange("(c p) t -> p c t", p=128)[:, :, t0:t0 + STW], yout_t[:], reads=yout, is_out=True)

    c.wait_all_outputs("sp")
    c.finalize()
    return nc


HD = 128
NQH = 16
NKV = 4
BS = 256
NBLK = 8
TFULL = 2048
NEG = -30000.0
SCALE = HD ** -0.5
NKB = [5, 6, 7, 8]


def _rope(c, ps, cosF, sinS, ts, tA, tB, out_ap, out_R):
    n = ts.stop - ts.start
    c.op("dve", "tensor_tensor", out=tA[:, 0:n], in0=ps[:, 0:n], in1=cosF[:, ts], op=ALU.mult, reads=[ps, cosF], writes=[tA])
    c.op("dve", "tensor_tensor", out=tB[0:64, 0:n], in0=ps[64:128, 0:n], in1=sinS[0:64, ts], op=ALU.mult, reads=[ps, sinS], writes=[tB])
    c.op("dve", "tensor_tensor", out=tB[64:128, 0:n], in0=ps[0:64, 0:n], in1=sinS[64:128, ts], op=ALU.mult, reads=[ps, sinS], writes=[tB])
    c.op("dve", "tensor_tensor", out=out_ap, in0=tA[:, 0:n], in1=tB[:, 0:n], op=ALU.add, reads=[tA, tB], writes=[out_R])


def _load_resid_and_norm(c, nc, pp, hT, g_d, ones_d):
    resid_t = c.sbuf("resid", [128, NC16, NT], F32)
    resid = [c.reg(resid_t[:, ch, :], "resid%d" % ch) for ch in range(NC16)]
    xn_t = c.sbuf("xn", [128, NC16, NT], BF16)
    xn = [[c.reg(xn_t[:, ch, tt * TT:(tt + 1) * TT], "xn") for tt in range(NTT)] for ch in range(NC16)]
    sq_t = c.sbuf("sqs", [128, NC16, NT], BF16)
    sq = [[c.reg(sq_t[:, ch, tt * TT:(tt + 1) * TT], "sq") for tt in range(NTT)] for ch in range(NC16)]
    gcol = c.reg(c.sbuf("gcol", [128, 16], F32), "gcol")
    ones = c.reg(c.sbuf("ones_sb", [128, 128], BF16), "ones")
    rstd = [c.reg(c.sbuf("rstd%d" % tt, [128, TT], F32), "rstd") for tt in range(NTT)]
    ntmp = c.reg(c.sbuf("ntmp", [128, TT], F32), "ntmp")
    c.dma("sp", gcol[:], g_d, writes=[gcol])
    c.dma("sp", ones[:], ones_d, writes=[ones])
    hv = hT.rearrange("(c p) t -> p c t", p=128)
    for ch in range(NC16):
        c.dma("sp" if ch % 2 == 0 else "act", resid[ch][:], hv[:, ch, :], writes=[resid[ch]])
    emit_rmsnorm(c, pp, resid, sq, ones, gcol, rstd, xn, ntmp)
    return xn, ones, sq_t, sq


def build_kv():
    nc = bass.Bass("TRN2", target_bir_lowering=False)

    def din(name, shape, dt=F32):
        return nc.dram_tensor(name, list(shape), dt, kind="ExternalInput").ap()

    hT = din("hT", [D, NT])
    g_d = din("g", [128, 16])
    wkv = din("wkv", [D, 1024])
    ones_d = din("ones", [128, 128], BF16)
    cos_d = din("cosF", [128, NT])
    sin_d = din("sinS", [128, NT])
    kT_o = nc.dram_tensor("kT", [NKV * HD, NT], BF16, kind="ExternalOutput").ap()
    v_o = nc.dram_tensor("v", [NT, NKV * HD], BF16, kind="ExternalOutput").ap()

    c = Ctx(nc)
    pp = PsumPool(c, 8)
    ws = WStream(c, nslot=2)
    xn, ones, sq_t, sq = _load_resid_and_norm(c, nc, pp, hT, g_d, ones_d)
    cosF = c.reg(c.sbuf("cosF_sb", [128, NT], F32), "cosF")
    sinS = c.reg(c.sbuf("sinS_sb", [128, NT], F32), "sinS")
    c.dma("act", cosF[:], cos_d, writes=[cosF])
    c.dma("act", sinS[:], sin_d, writes=[sinS])
    tA = c.reg(c.sbuf("tA", [128, TT], F32), "tA")
    tB = c.reg(c.sbuf("tB", [128, TT], F32), "tB")
    kout_t = c.sbuf("kout", [128, NKV, NT], BF16)
    kout = [c.reg(kout_t[:, g, :], "kout%d" % g) for g in range(NKV)]
    vout_t = c.sbuf("vout", [128, NT // 128, NKV * HD], BF16)
    vout = [c.reg(vout_t[:, i, :], "vout%d" % i) for i in range(NT // 128)]

    slot = ws.load(wkv, 0, 0)
    for g in range(NKV):
        for tt in range(NTT):
            ts = slice(tt * TT, (tt + 1) * TT)
            ps = pp.next()
            for k in range(NC16):
                c.op("pe", "matmul", out=ps[:], lhsT=slot[:, k, g * 128:(g + 1) * 128], rhs=xn[k][tt][:], start=(k == 0), stop=(k == NC16 - 1),
                     reads=[slot, xn[k][tt]], writes=[ps])
            _rope(c, ps, cosF, sinS, ts, tA, tB, kout[g][:, ts], kout[g])
    slot = ws.load(wkv, 0, 512)
    for i in range(NT // 128):
        tt, off = divmod(i * 128, TT)
        ps = pp.next()
        for k in range(NC16):
            c.op("pe", "matmul", out=ps[:], lhsT=xn[k][tt][:, off:off + 128], rhs=slot[:, k, :], start=(k == 0), stop=(k == NC16 - 1),
                 reads=[slot, xn[k][tt]], writes=[ps])
        c.op("act", "activation", out=vout[i][:], in_=ps[:], func=AF.Copy, reads=[ps], writes=[vout[i]])
    c.dma("sp", kT_o.rearrange("(g p) t -> p g t", p=128), kout_t[:], reads=kout, is_out=True)
    c.dma("act", v_o.rearrange("(i p) f -> p i f", p=128), vout_t[:], reads=vout, is_out=True)
    c.wait_all_outputs("sp")
    c.finalize()
    return nc


def build_mb():
    nc = bass.Bass("TRN2", target_bir_lowering=False)

    def din(name, shape, dt=F32):
        return nc.dram_tensor(name, list(shape), dt, kind="ExternalInput").ap()

    hT = din("hT", [D, NT])
    g_d = din("g", [128, 16])
    wq = din("wq", [D, D])
    ones_d = din("ones", [128, 128], BF16)
    ident_d = din("ident", [128, 128], BF16)
    cos_d = din("cosF", [128, NT])
    sin_d = din("sinS", [128, NT])
    kT_d = din("kTf", [NKV * HD, TFULL], BF16)
    v_d = din("vf", [TFULL, NKV * HD], BF16)
    negbig_d = din("negbig", [128, 4 * 128])
    vmul_d = din("vmul", [128, 4 * 128])
    cmask_d = din("cmask", [128, 4 * 8 * 2 * BS], BF16)
    oT = nc.dram_tensor("aT", [D, NT], BF16, kind="ExternalOutput").ap()

    c = Ctx(nc)
    pp = PsumPool(c, 3)
    accO = [c.reg(c.psum("accO%d" % i, [128, 512]), "accO%d" % i) for i in range(2)]
    accD = [c.reg(c.psum("accD%d" % i, [128, 512]), "accD%d" % i) for i in range(2)]
    ptb = c.reg(c.psum("ptb", [128, 1024], BF16), "ptb")
    ws = WStream(c, nslot=2)

    resid_t = c.sbuf("resid", [128, NC16, NT], F32)
    resid = [c.reg(resid_t[:, ch, :], "resid%d" % ch) for ch in range(NC16)]
    xn_t = c.sbuf("xn", [128, NC16, NT], BF16)
    xn = [[c.reg(xn_t[:, ch, tt * TT:(tt + 1) * TT], "xn") for tt in range(NTT)] for ch in range(NC16)]
    sq_t = c.sbuf("sqs", [128, NC16, NT], BF16)
    sq = [[c.reg(sq_t[:, ch, tt * TT:(tt + 1) * TT], "sq") for tt in range(NTT)] for ch in range(NC16)]
    gcol = c.reg(c.sbuf("gcol", [128, 16], F32), "gcol")
    ones = c.reg(c.sbuf("ones_sb", [128, 128], BF16), "ones")
    rstd = [c.reg(c.sbuf("rstd%d" % tt, [128, TT], F32), "rstd") for tt in range(NTT)]
    ntmp = c.reg(c.sbuf("ntmp", [128, TT], F32), "ntmp")
    c.dma("sp", gcol[:], g_d, writes=[gcol])
    c.dma("sp", ones[:], ones_d, writes=[ones])
    hv = hT.rearrange("(c p) t -> p c t", p=128)
    for ch in range(NC16):
        c.dma("sp" if ch % 2 == 0 else "act", resid[ch][:], hv[:, ch, :], writes=[resid[ch]])
    emit_rmsnorm(c, pp, resid, sq, ones, gcol, rstd, xn, ntmp)

    ident = c.reg(c.sbuf("ident_sb", [128, 128], BF16), "ident")
    c.dma("sp", ident[:], ident_d, writes=[ident])
    cosF = c.reg(c.sbuf("cosF_sb", [128, NT], F32), "cosF")
    sinS = c.reg(c.sbuf("sinS_sb", [128, NT], F32), "sinS")
    c.dma("act", cosF[:], cos_d, writes=[cosF])
    c.dma("act", sinS[:], sin_d, writes=[sinS])
    tA_t = c.sbuf("tA", [128, TT], F32)
    tB_t = c.sbuf("tB", [128, TT], F32)
    tA = c.reg(tA_t, "tA")
    tB = c.reg(tB_t, "tB")
    negbig = c.reg(c.sbuf("negbig_sb", [128, 512], F32), "negbig")
    vmul = c.reg(c.sbuf("vmul_sb", [128, 512], F32), "vmul")
    c.dma("sp", negbig[:], negbig_d, writes=[negbig])
    c.dma("sp", vmul[:], vmul_d, writes=[vmul])

    sq_all = [r for row in sq for r in row]
    kT_t = resid_t[:, 0:4, :].bitcast(BF16)
    V_t = resid_t[:, 4:8, :].rearrange("p c t -> p (c t)").bitcast(BF16).rearrange("p (i f) -> p i f", f=NKV * HD)
    kTs = [c.reg(kT_t[:, g, :], "kTs%d" % g) for g in range(NKV)]
    Vs = [c.reg(V_t[:, i, :], "Vs%d" % i) for i in range(TFULL // 128)]
    c.alias(kTs, resid[0:4])
    c.alias(Vs, resid[4:8])
    c.dma("sp", kT_t, kT_d.rearrange("(g p) t -> p g t", p=128), writes=kTs)
    c.dma("act", V_t, v_d.rearrange("(i p) f -> p i f", p=128), writes=Vs)

    qT_t = sq_t
    qT = [c.reg(qT_t[:, h, :], "qT%d" % h) for h in range(NQH)]
    for h in range(NQH):
        c.alias([qT[h]], sq[h])
    for hg in range(4):
        slot = ws.load(wq, 0, hg * 512)
        for hl in range(4):
            h = hg * 4 + hl
            for tt in range(NTT):
                ts = slice(tt * TT, (tt + 1) * TT)
                ps = pp.next()
                for k in range(NC16):
                    c.op("pe", "matmul", out=ps[:], lhsT=slot[:, k, hl * 128:(hl + 1) * 128], rhs=xn[k][tt][:], start=(k == 0), stop=(k == NC16 - 1),
                         reads=[slot, xn[k][tt]], writes=[ps])
                _rope(c, ps, cosF, sinS, ts, tA, tB, qT[h][:, ts], qT[h])

    ao_t = xn_t
    ao = [c.reg(ao_t[:, h, :], "ao%d" % h) for h in range(NQH)]
    for h in range(NQH):
        c.alias([ao[h]], xn[h])

    kmf = c.reg(c.sbuf("kmf", [128, NKV, NBLK], F32), "kmf")
    kmb = c.reg(c.sbuf("kmb", [128, NKV, NBLK], BF16), "kmb")
    for g in range(NKV):
        c.op("dve", "tensor_reduce", out=kmf[:, g, :], in_=kT_t[:, g, :].rearrange("p (n s) -> p n s", s=BS), axis=AX.X, op=ALU.add,
             reads=[kTs[g]], writes=[kmf])
    c.op("dve", "tensor_single_scalar", out=kmb[:], in_=kmf[:], scalar=1.0 / BS, op=ALU.mult, reads=[kmf], writes=[kmb])

    biasT_t = c.sbuf("biasT", [128, NT], BF16)
    biasT = [c.reg(biasT_t[:, i * BS:(i + 1) * BS], "biasT%d" % i) for i in range(4)]
    Gs = c.reg(c.sbuf("Gs", [128, 128], F32), "Gs")
    m8 = c.reg(c.sbuf("m8", [128, 128], F32), "m8")
    lt = c.reg(c.sbuf("lt", [128, 128], F32), "lt")
    bb = c.reg(c.sbuf("bb", [128, 128], BF16), "bb")
    for qt in range(NT // 128):
        i = qt // 2
        tsl = slice(qt * 128, (qt + 1) * 128)
        pg = pp.next()
        for h in range(NQH):
            c.op("pe", "matmul", out=pg[:, h * 8:(h + 1) * 8], lhsT=qT[h][:, tsl], rhs=kmb[:, h // 4, :], start=True, stop=True,
                 reads=[qT[h], kmb], writes=[pg])
        c.op("dve", "tensor_tensor", out=Gs[:], in0=pg[:, 0:128], in1=negbig[:, i * 128:(i + 1) * 128], op=ALU.add, reads=[pg, negbig], writes=[Gs])
        for h in range(NQH):
            c.op("dve", "max", out=m8[:, h * 8:(h + 1) * 8], in_=Gs[:, h * 8:(h + 1) * 8], reads=[Gs], writes=[m8])
        c.op("dve", "tensor_tensor", out=lt[:].rearrange("p (h n) -> p h n", n=8), in0=Gs[:].rearrange("p (h n) -> p h n", n=8),
             in1=m8[:].rearrange("p (h n) -> p h n", n=8)[:, :, 2:3].to_broadcast([128, NQH, 8]), op=ALU.is_lt, reads=[Gs, m8], writes=[lt])
        c.op("dve", "tensor_tensor", out=bb[:], in0=lt[:], in1=vmul[:, i * 128:(i + 1) * 128], op=ALU.mult, reads=[lt, vmul], writes=[bb])
        c.op("pe", "transpose", out=ptb[:, 0:128], in_=bb[:], identity=ident[:], reads=[bb, ident], writes=[ptb])
        c.op("act", "activation", out=biasT_t[:, tsl], in_=ptb[:, 0:128], func=AF.Copy, reads=[ptb], writes=[biasT[i]])

    cmv = cmask_d.rearrange("p (i j k t) -> p i j k t", i=4, j=8, k=2)
    plan = {}
    for i in range(4):
        for j in range(NKB[i]):
            qbs = (i, 4 + i)
            need_gate = any(j < qb for qb in qbs)
            need_const = any(j >= qb for qb in qbs)
            cmr = None
            if need_const:
                cmr = c.reg(c.sbuf("cm_%d_%d" % (i, j), [128, 2, BS], BF16), "cm_%d_%d" % (i, j))
                c.dma("sp", cmr[:], cmv[:, i, j, :, :], writes=[cmr])
            plan[(i, j)] = (need_gate, cmr)

    tAb = tA_t[:, :].bitcast(BF16)
    Pt = [c.reg(tAb[:, k * BS:(k + 1) * BS], "Pt%d" % k) for k in range(3)]
    rden = c.reg(tB_t[:, 0:BS], "rden")
    c.alias(Pt, [tA])
    c.alias([rden], [tB])
    pk = 0
    it = 0
    for i in range(4):
        qs = slice(i * BS, (i + 1) * BS)
        for h in range(NQH):
            g = h // 4
            aO, aD = accO[it % 2], accD[it % 2]
            it += 1
            ntile = NKB[i] * 2
            n = 0
            for j in range(NKB[i]):
                need_gate, cmr = plan[(i, j)]
                for kt in range(2):
                    ks = slice(j * BS + kt * 128, j * BS + kt * 128 + 128)
                    st_ = pp.next()
                    last_mm = "main"
                    if cmr is not None:
                        last_mm = "const"
                    elif need_gate:
                        last_mm = "gate"
                    c.op("pe", "matmul", out=st_[:, 0:BS], lhsT=kTs[g][:, ks], rhs=qT[h][:, qs], start=True, stop=(last_mm == "main"), reads=[kTs[g], qT[h]], writes=[st_])
                    if need_gate:
                        col = h * 8 + j
                        c.op("pe", "matmul", out=st_[:, 0:BS], lhsT=ident[:, col:col + 1].to_broadcast([128, 128]), rhs=biasT_t[:, qs], start=False, stop=(last_mm == "gate"),
                             reads=[ident, biasT[i]], writes=[st_])
                    if cmr is not None:
                        c.op("pe", "matmul", out=st_[:, 0:BS], lhsT=ident[:], rhs=cmr[:, kt, :], start=False, stop=True, reads=[ident, cmr], writes=[st_])
                    p_ = Pt[pk % 3]
                    pk += 1
                    c.op("act", "activation", out=p_[:], in_=st_[:, 0:BS], func=AF.Exp, scale=SCALE, reads=[st_], writes=[p_])
                    vi = (j * BS + kt * 128) // 128
                    c.op("pe", "matmul", out=aO[:, 0:BS], lhsT=V_t[:, vi, g * 128:(g + 1) * 128], rhs=p_[:], start=(n == 0), stop=(n == ntile - 1), reads=[Vs[vi], p_], writes=[aO])
                    c.op("pe", "matmul", out=aD[:, 0:BS], lhsT=ones[:], rhs=p_[:], start=(n == 0), stop=(n == ntile - 1), reads=[ones, p_], writes=[aD])
                    n += 1
            c.op("dve", "reciprocal", out=rden[:], in_=aD[:, 0:BS], reads=[aD], writes=[rden])
            c.op("dve", "tensor_tensor", out=ao[h][:, qs], in0=aO[:, 0:BS], in1=rden[:], op=ALU.mult, reads=[aO, rden], writes=[ao[h]])
    c.dma("sp", oT.rearrange("(h p) t -> p h t", p=128), ao_t[:], reads=ao, is_out=True)
    c.wait_all_outputs("sp")
    c.finalize()
    return nc


import ml_dtypes
from concourse.bass_utils import run_bass_kernel_spmd

_BF = ml_dtypes.bfloat16
_NCORES = 8
_PROGS = {}


def _prog(key, fn):
    if key not in _PROGS:
        _PROGS[key] = fn()
    return _PROGS[key]


def _lay(vec, n):
    return np.ascontiguousarray(np.asarray(vec, np.float32).reshape(n, 128).T)


def _consts_rw():
    su = np.triu(np.ones((128, 128), np.float32), 1)
    ui = np.triu(np.ones((128, 128), np.float32), 0)
    sl_ = np.tril(np.ones((128, 128), np.float32), -1)
    bones = np.zeros((128, 128), np.float32)
    bones[:64, :64] = 1
    bones[64:, 64:] = 1
    return {"ident": np.eye(128, dtype=np.float32).astype(_BF), "onesb": np.ones((128, 128), _BF), "bones": bones,
            "onesf": np.ones((128, 128), np.float32), "mask3": np.concatenate([su, ui, ui], 1).astype(_BF),
            "masksu": su.astype(_BF), "masksl": sl_.astype(_BF)}


def _rope_tables(pos):
    inv = 10000.0 ** (-np.arange(64, dtype=np.float64) / 64)
    ang = pos[None, :].astype(np.float64) * inv[:, None]
    cosF = np.concatenate([np.cos(ang), np.cos(ang)], 0).astype(np.float32)
    sinS = np.concatenate([-np.sin(ang), np.sin(ang)], 0).astype(np.float32)
    return cosF, sinS


def _mb_tables(half):
    negbig = np.zeros((4, NQH, 8), np.float32)
    vmul = np.zeros((4, NQH, 8), np.float32)
    cmask = np.zeros((128, 4, 8, 2, BS), np.float32)
    for i in range(4):
        qb = half * 4 + i
        negbig[i, :, qb:] = -1e30
        vmul[i, :, :qb] = NEG
        for j in range(8):
            for kt in range(2):
                s_glob = j * BS + kt * 128 + np.arange(128)[:, None]
                t_glob = qb * BS + np.arange(BS)[None, :]
                if j == qb:
                    cmask[:, i, j, kt] = np.where(s_glob <= t_glob, 0.0, NEG)
                elif j > qb:
                    cmask[:, i, j, kt] = NEG
    return (np.tile(negbig.reshape(1, 512), (128, 1)), np.tile(vmul.reshape(1, 512), (128, 1)), cmask.reshape(128, -1).astype(_BF))


def _launch(nc, in_maps):
    res = run_bass_kernel_spmd(nc, in_maps, core_ids=list(range(_NCORES)))
    return res.results


def kernel(x, ln_mix_g, ln_ffn_g, w_ff1, w_ff2, rw_mu, rw_w_rkv, rw_w0, rw_w1, rw_w2, rw_a0, rw_a1, rw_a2,
           rw_g1, rw_g2, rw_k_k, rw_k_a, rw_r_k, rw_gn_w, rw_gn_b, rw_w_o, rw_v0, rw_v1, rw_v2,
           kv_norm_g, w_kv, mb_w_q, mb_w_o, final_g):
    f32 = np.float32
    A = lambda a: np.ascontiguousarray(np.asarray(a, f32))
    x = A(x)
    B, T, Dm = x.shape
    hT = [np.ascontiguousarray(x[b].T) for b in range(B)]
    ones_bf = np.ones((128, 128), _BF)
    ident_bf = np.eye(128, dtype=f32).astype(_BF)
    crw = _consts_rw()
    vT_first = None

    def run_ff(layer, mT_cores, w_o):
        final = (layer == 3)
        nc = _prog(("ff", final), lambda: build_ff(final=final))
        g = np.concatenate([_lay(ln_ffn_g[layer], 16), _lay(final_g, 16)], 1)
        w1, w2, wo = A(w_ff1[layer]), A(w_ff2[layer]), A(w_o)
        maps = []
        for c in range(_NCORES):
            b, s = divmod(c, 2)
            maps.append({"hT": np.ascontiguousarray(hT[b][:, s * NT:(s + 1) * NT]), "mT": mT_cores[c], "wo": wo, "w1": w1, "w2": w2,
                         "g": g, "ones": ones_bf})
        out = _launch(nc, maps)
        for b in range(B):
            hT[b] = np.ascontiguousarray(np.concatenate([out[2 * b]["oT"], out[2 * b + 1]["oT"]], 1))

    for i in range(2):
        nc = _prog(("rw", i), lambda: build_rw(i, T))
        maps = []
        for c in range(_NCORES):
            b, s = divmod(c, 2)
            sl = slice(s * FL, (s + 1) * FL)
            rk = np.asarray(rw_r_k[i], f32).reshape(-1)
            v0 = np.asarray(rw_v0[0], f32) if i == 1 else np.zeros(Dm, f32)
            pvv = np.concatenate([_lay(rw_w0[i][sl], 8), _lay(rw_a0[i][sl], 8), _lay(v0[sl], 8), _lay(rw_k_k[i][sl], 8), _lay(rw_k_a[i][sl], 8),
                                  _lay(rk[sl], 8), _lay(rw_gn_w[i][sl], 8), _lay(rw_gn_b[i][sl], 8),
                                  np.zeros((128, 8), np.float32)], 1)
            d = {"hT": hT[b], "gmix": _lay(ln_mix_g[i], 16), "mu": np.concatenate([_lay(rw_mu[i][k], 16) for k in range(6)], 1),
                 "wr": A(rw_w_rkv[i][0][:, sl]), "wk": A(rw_w_rkv[i][1][:, sl]), "wv": A(rw_w_rkv[i][2][:, sl]),
                 "w1": A(rw_w1[i]), "a1": A(rw_a1[i]), "g1": A(rw_g1[i]),
                 "w2": A(rw_w2[i][:, sl]), "a2": A(rw_a2[i][:, sl]), "g2": A(rw_g2[i][:, sl]), "pv": pvv}
            if i == 1:
                d.update({"v1": A(rw_v1[0]), "v2": A(rw_v2[0][:, sl]), "vfT": vT_first[c]})
            d.update(crw)
            maps.append(d)
        out = _launch(nc, maps)
        if i == 0:
            vT_first = [out[c]["vT"] for c in range(_NCORES)]
        mT = []
        for c in range(_NCORES):
            b, s = divmod(c, 2)
            full = np.concatenate([out[2 * b]["ygT"], out[2 * b + 1]["ygT"]], 0)
            mT.append(np.ascontiguousarray(full[:, s * NT:(s + 1) * NT]))
        run_ff(i, mT, rw_w_o[i])

    nc = _prog(("kv",), build_kv)
    maps = []
    for c in range(_NCORES):
        b, s = divmod(c, 2)
        cosF, sinS = _rope_tables(np.arange(s * NT, (s + 1) * NT))
        maps.append({"hT": np.ascontiguousarray(hT[b][:, s * NT:(s + 1) * NT]), "g": _lay(kv_norm_g, 16), "wkv": A(w_kv), "ones": ones_bf,
                     "cosF": cosF, "sinS": sinS})
    out = _launch(nc, maps)
    kTf = [np.ascontiguousarray(np.concatenate([out[2 * b]["kT"], out[2 * b + 1]["kT"]], 1)) for b in range(B)]
    vf = [np.ascontiguousarray(np.concatenate([out[2 * b]["v"], out[2 * b + 1]["v"]], 0)) for b in range(B)]

    for j in range(2):
        layer = 2 + j
        nc = _prog(("mb",), build_mb)
        maps = []
        for c in range(_NCORES):
            b, s = divmod(c, 2)
            cosF, sinS = _rope_tables(np.arange(s * NT, (s + 1) * NT))
            negbig, vmul, cmask = _mb_tables(s)
            maps.append({"hT": np.ascontiguousarray(hT[b][:, s * NT:(s + 1) * NT]), "g": _lay(ln_mix_g[layer], 16), "wq": A(mb_w_q[j]),
                         "ones": ones_bf, "ident": ident_bf, "cosF": cosF, "sinS": sinS, "kTf": kTf[b], "vf": vf[b],
                         "negbig": negbig, "vmul": vmul, "cmask": cmask})
        out = _launch(nc, maps)
        run_ff(layer, [out[c]["aT"] for c in range(_NCORES)], mb_w_o[j])

    return np.ascontiguousarray(np.stack([hT[b].T for b in range(B)], 0)).astype(np.float32)
```
